# Optimizing a Trainium2 kernel written in Bass

```python
import math
import jax, jax.numpy as jnp
from jax import lax
import numpy as np

D_MODEL = 1024
BATCH = 8
SEQ = 8192
DEPTH = 2

N_META = 16
CHUNK = 64
PAD = CHUNK - N_META
N_MIXERS = 2
RMS_EPS = 1e-6
RET_HEADS = 4
RET_DK = 256
RET_DV = 512
RET_QK = RET_HEADS * RET_DK
RET_V = RET_HEADS * RET_DV
RET_IN = 2 * RET_QK + 2 * RET_V
ROPE_BASE = 10000.0
DN_HEADS = 8
DN_DK = 128
DN_DV = 256
DN_QK = DN_HEADS * DN_DK
DN_V = DN_HEADS * DN_DV
DN_CONV_CH = 2 * DN_QK + DN_V
DN_IN = DN_CONV_CH + DN_V + 2 * DN_HEADS
CONV_K = 4
FFN_HIDDEN = ((8 * D_MODEL + 3 * 256 - 1) // (3 * 256)) * 256
N_RET_LAYERS = (DEPTH + 1) // 2
N_DN_LAYERS = DEPTH // 2

kernel_name = "hybrid_retention_gated_deltanet_meta"


def rmsnorm(x, w):
    xf = x.astype(jnp.float32)
    y = xf * lax.rsqrt(jnp.mean(xf * xf, axis=-1, keepdims=True) + RMS_EPS)
    return (y * w.astype(jnp.float32)).astype(x.dtype)


def l2norm(x):
    return x * lax.rsqrt(jnp.sum(x * x, axis=-1, keepdims=True) + RMS_EPS)


def rope(t, pos):
    half = t.shape[-1] // 2
    inv_freq = ROPE_BASE ** (-jnp.arange(half, dtype=jnp.float32) / half)
    ang = pos[:, None] * inv_freq[None, :]
    cos = jnp.cos(ang)[None, :, None, :]
    sin = jnp.sin(ang)[None, :, None, :]
    t1, t2 = t[..., :half], t[..., half:]
    return jnp.concatenate([t1 * cos - t2 * sin, t1 * sin + t2 * cos], axis=-1)


def to_chunks(t):
    b, l, h, d = t.shape
    return t.reshape(b, l // CHUNK, CHUNK, h, d).transpose(1, 0, 3, 2, 4)


def from_chunks(t):
    n, b, h, c, d = t.shape
    return t.transpose(1, 0, 3, 2, 4).reshape(b, n * c, h, d)


def causal_conv(x, w):
    ch = x.shape[-1]
    return lax.conv_general_dilated(
        x, w.astype(x.dtype)[:, None, :], window_strides=(1,), padding=[(CONV_K - 1, 0)],
        dimension_numbers=("NWC", "WIO", "NWC"), feature_group_count=ch)


def gated_head_norm(o, norm_w, gate):
    o = o * lax.rsqrt(jnp.mean(o * o, axis=-1, keepdims=True) + RMS_EPS) * norm_w.astype(jnp.float32)
    return o * jax.nn.silu(gate.astype(jnp.float32))


def retention(h, w_in, gn_w, w_out, valid, pos):
    b, l, _ = h.shape
    proj = h @ w_in
    q, k, v, g = jnp.split(proj, [RET_QK, 2 * RET_QK, 2 * RET_QK + RET_V], axis=-1)
    q = rope(q.reshape(b, l, RET_HEADS, RET_DK).astype(jnp.float32), pos)
    k = rope(k.reshape(b, l, RET_HEADS, RET_DK).astype(jnp.float32), pos)
    k = k * (RET_DK ** -0.5) * valid[None, :, None, None]
    v = v.reshape(b, l, RET_HEADS, RET_DV).astype(jnp.float32)

    log_gamma = jnp.log1p(-jnp.exp2(-5.0 - jnp.arange(RET_HEADS, dtype=jnp.float32)))
    idx = jnp.arange(CHUNK, dtype=jnp.float32)
    rel = idx[:, None] - idx[None, :]
    dmask = jnp.where((rel >= 0)[None], jnp.exp(log_gamma[:, None, None] * jnp.maximum(rel, 0.0)), 0.0)
    xi = jnp.exp(log_gamma[:, None] * (idx[None, :] + 1.0))[:, :, None]
    zeta = jnp.exp(log_gamma[:, None] * (CHUNK - 1.0 - idx[None, :]))[:, :, None]
    gamma_c = jnp.exp(log_gamma * CHUNK)[:, None, None]

    def step(state, inp):
        qc, kc, vc = inp
        scores = jnp.einsum("bhid,bhjd->bhij", qc, kc) * dmask
        o = jnp.einsum("bhij,bhjv->bhiv", scores, vc) + jnp.einsum("bhid,bhdv->bhiv", qc * xi, state)
        state = gamma_c * state + jnp.einsum("bhjd,bhjv->bhdv", kc * zeta, vc)
        return state, o

    s0 = jnp.zeros((b, RET_HEADS, RET_DK, RET_DV), jnp.float32)
    _, o = lax.scan(step, s0, (to_chunks(q), to_chunks(k), to_chunks(v)))
    o = from_chunks(o)
    o = gated_head_norm(o, gn_w, g.reshape(b, l, RET_HEADS, RET_DV))
    return o.reshape(b, l, RET_V).astype(h.dtype) @ w_out


def gated_deltanet(h, w_in, conv_w, a_log, dt_bias, norm_w, w_out, valid):
    b, l, _ = h.shape
    proj = h @ w_in
    qkv, gate, beta_in, a_in = jnp.split(
        proj, [DN_CONV_CH, DN_CONV_CH + DN_V, DN_CONV_CH + DN_V + DN_HEADS], axis=-1)
    qkv = qkv * valid[None, :, None].astype(qkv.dtype)
    qkv = jax.nn.silu(causal_conv(qkv, conv_w))
    q, k, v = jnp.split(qkv, [DN_QK, 2 * DN_QK], axis=-1)
    q = l2norm(q.reshape(b, l, DN_HEADS, DN_DK).astype(jnp.float32)) * (DN_DK ** -0.5)
    k = l2norm(k.reshape(b, l, DN_HEADS, DN_DK).astype(jnp.float32))
    v = v.reshape(b, l, DN_HEADS, DN_DV).astype(jnp.float32)
    vmask = valid[None, :, None]
    beta = (jax.nn.sigmoid(beta_in.astype(jnp.float32)) * vmask)[..., None]
    g = (-jnp.exp(a_log.astype(jnp.float32))
         * jax.nn.softplus(a_in.astype(jnp.float32) + dt_bias.astype(jnp.float32)) * vmask)[..., None]

    incl = jnp.tril(jnp.ones((CHUNK, CHUNK), dtype=bool))
    strict = jnp.tril(jnp.ones((CHUNK, CHUNK), dtype=bool), -1)
    eye = jnp.eye(CHUNK, dtype=jnp.float32)

    def step(state, inp):
        qc, kc, vc, bc, gc = inp
        gam = jnp.cumsum(gc, axis=-2)
        diff = gam - jnp.swapaxes(gam, -1, -2)
        decay = jnp.exp(jnp.where(incl, diff, -jnp.inf))
        kk = jnp.einsum("bhid,bhjd->bhij", kc, kc)
        a_mat = jnp.where(strict, bc * kk * decay, 0.0)
        rhs = jnp.concatenate([vc * bc, kc * bc * jnp.exp(gam)], axis=-1)
        sol = lax.linalg.triangular_solve(eye + a_mat, rhs, left_side=True, lower=True,
                                          unit_diagonal=True)
        u, w = sol[..., :DN_DV], sol[..., DN_DV:]
        v_new = u - jnp.einsum("bhik,bhkv->bhiv", w, state)
        qk = jnp.einsum("bhid,bhjd->bhij", qc, kc) * decay
        o = (jnp.einsum("bhid,bhdv->bhiv", qc * jnp.exp(gam), state)
             + jnp.einsum("bhij,bhjv->bhiv", qk, v_new))
        g_last = gam[..., -1:, :]
        state = state * jnp.exp(g_last) + jnp.einsum(
            "bhjd,bhjv->bhdv", kc * jnp.exp(g_last - gam), v_new)
        return state, o

    s0 = jnp.zeros((b, DN_HEADS, DN_DK, DN_DV), jnp.float32)
    _, o = lax.scan(step, s0, (to_chunks(q), to_chunks(k), to_chunks(v),
                               to_chunks(beta), to_chunks(g)))
    o = from_chunks(o)
    o = gated_head_norm(o, norm_w, gate.reshape(b, l, DN_HEADS, DN_DV))
    return o.reshape(b, l, DN_V).astype(h.dtype) @ w_out


def swiglu(h, w_gate, w_up, w_down):
    return (jax.nn.silu(h @ w_gate) * (h @ w_up)) @ w_down


def setup_inputs(seed: int = 0) -> dict:
    key = jax.random.key(seed)
    ks = jax.random.split(key, 20)
    f32 = jnp.float32
    nrm = lambda k, shape, fan_in: jax.random.normal(k, shape, f32) * (fan_in ** -0.5)
    gain = lambda k, shape: 1.0 + 0.02 * jax.random.normal(k, shape, f32)
    dt = jnp.exp(jax.random.uniform(ks[9], (N_DN_LAYERS, DN_HEADS), f32)
                 * (math.log(0.1) - math.log(0.001)) + math.log(0.001))
    return {
        "x": jax.random.normal(ks[0], (BATCH, SEQ, D_MODEL), f32),
        "meta_tokens": jax.random.normal(ks[1], (N_META, D_MODEL), f32),
        "mix_norm_w": gain(ks[2], (DEPTH, D_MODEL)),
        "ffn_norm_w": gain(ks[3], (DEPTH, D_MODEL)),
        "ret_w_in": nrm(ks[4], (N_RET_LAYERS, D_MODEL, RET_IN), D_MODEL),
        "ret_gn_w": gain(ks[5], (N_RET_LAYERS, RET_DV)),
        "ret_w_out": nrm(ks[6], (N_RET_LAYERS, RET_V, D_MODEL), RET_V),
        "dn_w_in": nrm(ks[7], (N_DN_LAYERS, D_MODEL, DN_IN), D_MODEL),
        "dn_conv_w": nrm(ks[8], (N_DN_LAYERS, CONV_K, DN_CONV_CH), CONV_K),
        "dn_a_log": jnp.log(jax.random.uniform(ks[10], (N_DN_LAYERS, DN_HEADS), f32, 1.0, 16.0)),
        "dn_dt_bias": dt + jnp.log(-jnp.expm1(-dt)),
        "dn_norm_w": gain(ks[11], (N_DN_LAYERS, DN_DV)),
        "dn_w_out": nrm(ks[12], (N_DN_LAYERS, DN_V, D_MODEL), DN_V),
        "ffn_w_gate": nrm(ks[13], (DEPTH, D_MODEL, FFN_HIDDEN), D_MODEL),
        "ffn_w_up": nrm(ks[14], (DEPTH, D_MODEL, FFN_HIDDEN), D_MODEL),
        "ffn_w_down": nrm(ks[15], (DEPTH, FFN_HIDDEN, D_MODEL), FFN_HIDDEN),
        "final_norm_w": gain(ks[16], (D_MODEL,)),
    }


def reference(x, meta_tokens, mix_norm_w, ffn_norm_w, ret_w_in, ret_gn_w, ret_w_out,
              dn_w_in, dn_conv_w, dn_a_log, dn_dt_bias, dn_norm_w, dn_w_out,
              ffn_w_gate, ffn_w_up, ffn_w_down, final_norm_w):
    b = x.shape[0]
    h = jnp.concatenate([
        jnp.zeros((b, PAD, D_MODEL), x.dtype),
        jnp.broadcast_to(meta_tokens.astype(x.dtype)[None], (b, N_META, D_MODEL)),
        x], axis=1)
    l = h.shape[1]
    pos_i = jnp.arange(l) - PAD
    valid = (pos_i >= 0).astype(jnp.float32)
    pos = pos_i.astype(jnp.float32)
    for i in range(DEPTH):
        hn = rmsnorm(h, mix_norm_w[i])
        if i % N_MIXERS == 0:
            j = i // N_MIXERS
            mix = retention(hn, ret_w_in[j], ret_gn_w[j], ret_w_out[j], valid, pos)
        else:
            j = i // N_MIXERS
            mix = gated_deltanet(hn, dn_w_in[j], dn_conv_w[j], dn_a_log[j], dn_dt_bias[j],
                                 dn_norm_w[j], dn_w_out[j], valid)
        h = h + mix
        h = h + swiglu(rmsnorm(h, ffn_norm_w[i]), ffn_w_gate[i], ffn_w_up[i], ffn_w_down[i])
    return rmsnorm(h, final_norm_w)[:, CHUNK:, :]
```

```python
import numpy as np
from contextlib import ExitStack
import concourse.bass as bass
import concourse.mybir as mybir
from concourse.bass_utils import run_bass_kernel_spmd

F32 = mybir.dt.float32
BF16 = mybir.dt.bfloat16
AF = mybir.ActivationFunctionType
ALU = mybir.AluOpType

T = 256
NB = T // 128
NCH = T // 64
LPAD = 240
D = 1024
FF = 2816
NHC = FF // 128
EPS = 1e-6
NEG = -30000.0
SELF_SYNC = True


def _esz(dt):
    return 2 if dt == BF16 else 4


class Region:
    __slots__ = ("w", "r")

    def __init__(self):
        self.w = None
        self.r = {}


class Mem:
    def __init__(self, handles, nbytes, gran):
        self.h = handles
        self.gran = gran
        ng = (nbytes + gran - 1) // gran
        self.regs = [[Region() for _ in range(ng)] for _ in range(2)]

    def view(self, off, dt, shape, p0=0, p1=128):
        e = _esz(dt)
        assert off % e == 0
        n = int(np.prod(shape))
        ap = self.h[dt][p0:p1, off // e: off // e + n]
        if len(shape) == 2:
            ap = ap.rearrange("p (a b) -> p a b", a=shape[0])
        elif len(shape) == 3:
            ap = ap.rearrange("p (a b c) -> p a b c", a=shape[0], b=shape[1])
        return V(ap, self)


class V:
    __slots__ = ("ap", "mem")

    def __init__(self, ap, mem):
        self.ap = ap
        self.mem = mem

    def __getitem__(self, idx):
        return V(self.ap[idx], self.mem)

    def rr(self, pat, **kw):
        return V(self.ap.rearrange(pat, **kw), self.mem)

    def bc(self, shape):
        return V(self.ap.broadcast_to(list(shape)), self.mem)

    def us(self, axis):
        return V(self.ap.unsqueeze(axis), self.mem)

    def regions(self):
        ap = self.ap
        e = _esz(ap.dtype)
        pat = ap.ap
        rowlen = pat[0][0]
        col = ap.offset % rowlen if rowlen > 0 else ap.offset
        ext = 1
        for (s, c) in pat[1:]:
            ext += (c - 1) * abs(s)
        b0 = col * e
        b1 = (col + ext) * e
        g = self.mem.gran
        pstart = ap.offset // rowlen if rowlen > 0 else 0
        pend = pstart + pat[0][1]
        out = []
        if pstart < 64:
            out += self.mem.regs[0][b0 // g: (b1 - 1) // g + 1]
        if pend > 64:
            out += self.mem.regs[1][b0 // g: (b1 - 1) // g + 1]
        return out


def _ap(x):
    return x.ap if isinstance(x, V) else x


class EngState:
    def __init__(self, name, eng, sem):
        self.name, self.eng, self.sem = name, eng, sem
        self.count = 0
        self.seen = {}


class Sched:
    def __init__(self, nc, stack):
        self.nc = nc
        self.stack = stack
        self.E = {}
        for name, eng in (("pe", nc.tensor), ("act", nc.scalar), ("dve", nc.vector),
                          ("pool", nc.gpsimd), ("sp", nc.sync)):
            sem = stack.enter_context(nc.semaphore("s_" + name))
            self.E[name] = EngState(name, eng, sem)
        self.dsem = {}
        self.n_ins = 0

    def sem_of(self, key):
        if isinstance(key, tuple):
            return self.dsem[key[1]][0]
        return self.E[key].sem

    def _wait(self, E, key, val):
        if E.seen.get(key, 0) >= val:
            return
        if key == E.name:
            if E.name in ("pe", "sp") or not SELF_SYNC:
                return
        E.eng.wait_ge(self.sem_of(key), val)
        E.seen[key] = val

    def _deps(self, E, reads, writes):
        deps = {}
        for v in reads:
            for rg in v.regions():
                if rg.w is not None:
                    k, val = rg.w
                    if deps.get(k, 0) < val:
                        deps[k] = val
        for v in writes:
            for rg in v.regions():
                if rg.w is not None:
                    k, val = rg.w
                    if deps.get(k, 0) < val:
                        deps[k] = val
                for k, val in rg.r.items():
                    if deps.get(k, 0) < val:
                        deps[k] = val
        for k, val in deps.items():
            self._wait(E, k, val)

    def op(self, en, fn, reads=(), writes=()):
        E = self.E[en]
        reads = [v for v in reads if isinstance(v, V)]
        writes = [v for v in writes if isinstance(v, V)]
        self._deps(E, reads, writes)
        ins = fn(E.eng)
        E.count += 1
        ins.then_inc(E.sem, 1)
        self.n_ins += 1
        key = E.name
        for v in reads:
            for rg in v.regions():
                rg.r[key] = E.count
        for v in writes:
            for rg in v.regions():
                rg.w = (key, E.count)
                rg.r = {}
        return ins

    def dma(self, qn, semname, pairs, reads=(), writes=(), **kw):
        Q = self.E[qn]
        if semname not in self.dsem:
            self.dsem[semname] = [self.stack.enter_context(self.nc.semaphore("d_" + semname)), 0]
        ds = self.dsem[semname]
        reads = [v for v in reads if isinstance(v, V)]
        writes = [v for v in writes if isinstance(v, V)]
        self._deps(Q, reads, writes)
        for (o, i) in pairs:
            Q.eng.dma_start(out=_ap(o), in_=_ap(i), **kw).then_inc(ds[0], 16)
            ds[1] += 16
        key = ("d", semname)
        for v in reads:
            for rg in v.regions():
                rg.r[key] = ds[1]
        for v in writes:
            for rg in v.regions():
                rg.w = (key, ds[1])
                rg.r = {}

    def wait_dma_all(self, en, semname):
        E = self.E[en]
        ds = self.dsem[semname]
        self._wait(E, ("d", semname), ds[1])

    def mm(self, out, lhsT, rhs, start=True, stop=True):
        return self.op("pe", lambda e: e.matmul(_ap(out), lhsT=_ap(lhsT), rhs=_ap(rhs), start=start, stop=stop),
                       reads=[lhsT, rhs], writes=[out])

    def tr(self, out, in_, ident):
        return self.op("pe", lambda e: e.transpose(out=_ap(out), in_=_ap(in_), identity=_ap(ident)),
                       reads=[in_, ident], writes=[out])

    def act(self, out, in_, func, bias=None, scale=None, accum_out=None, en="act"):
        kw = {}
        rd = [in_]
        if bias is not None:
            kw["bias"] = _ap(bias) if isinstance(bias, V) else bias
            rd.append(bias)
        if scale is not None:
            kw["scale"] = _ap(scale) if isinstance(scale, V) else scale
            rd.append(scale)
        wr = [out]
        if accum_out is not None:
            kw["accum_out"] = _ap(accum_out)
            wr.append(accum_out)
        return self.op("act", lambda e: e.activation(out=_ap(out), in_=_ap(in_), func=func, **kw),
                       reads=rd, writes=wr)

    def tt(self, en, out, in0, in1, op):
        return self.op(en, lambda e: e.tensor_tensor(out=_ap(out), in0=_ap(in0), in1=_ap(in1), op=op),
                       reads=[in0, in1], writes=[out])

    def ts(self, en, out, in0, s1, op0, s2=None, op1=None):
        rd = [in0, s1, s2]
        a1 = _ap(s1) if isinstance(s1, V) else s1
        a2 = _ap(s2) if isinstance(s2, V) else s2
        if op1 is None:
            return self.op(en, lambda e: e.tensor_scalar(out=_ap(out), in0=_ap(in0), scalar1=a1, scalar2=None, op0=op0),
                           reads=rd, writes=[out])
        return self.op(en, lambda e: e.tensor_scalar(out=_ap(out), in0=_ap(in0), scalar1=a1, scalar2=a2, op0=op0, op1=op1),
                       reads=rd, writes=[out])

    def stt(self, out, in0, scalar, in1, op0, op1):
        sc = _ap(scalar) if isinstance(scalar, V) else scalar
        return self.op("dve", lambda e: e.scalar_tensor_tensor(out=_ap(out), in0=_ap(in0), scalar=sc, in1=_ap(in1),
                                                                op0=op0, op1=op1),
                       reads=[in0, scalar, in1], writes=[out])

    def copy(self, en, out, in_):
        if en == "act":
            return self.act(out, in_, AF.Copy)
        return self.op(en, lambda e: e.tensor_copy(out=_ap(out), in_=_ap(in_)), reads=[in_], writes=[out])

    def memset(self, en, out, val):
        return self.op(en, lambda e: e.memset(_ap(out), val), writes=[out])

    def reduce_add(self, out, in_):
        return self.op("dve", lambda e: e.tensor_reduce(out=_ap(out), in_=_ap(in_), axis=mybir.AxisListType.X, op=ALU.add),
                       reads=[in_], writes=[out])


RET_H, RET_DK, RET_DV = 4, 256, 512
DN_H, DN_DK, DN_DV = 8, 128, 256


def _gammas():
    return (1.0 - np.exp2(-5.0 - np.arange(RET_H, dtype=np.float64)))


C_ID, C_ONE, C_TLE, C_TGT = 0, 128, 256, 320
C_MT, C_MS = 384, 896
C_ZETA = 1408
C_VAL = C_ZETA + NB * 4
C_W32 = C_VAL + NB
C_D, C_XI = 0, 4 * NB * 128
C_W16 = C_XI + 4 * T


def host_consts(NT):
    g = _gammas()
    c32 = np.zeros((128, C_W32), np.float32)
    c32[:, C_ID:C_ID + 128] = np.eye(128)
    c32[:, C_ONE:C_ONE + 128] = 1.0
    k = np.arange(128) % 64
    i = np.arange(64)
    c32[:, C_TLE:C_TLE + 64] = (k[:, None] <= i[None, :])
    c32[:, C_TGT:C_TGT + 64] = (i[None, :] < k[:, None])
    mt = np.where(i[None, :] < k[:, None], NEG, 0.0)
    ms = np.where(i[None, :] >= k[:, None], NEG, 0.0)
    c32[:, C_MT:C_MT + 512] = np.tile(mt, (1, 8))
    c32[:, C_MS:C_MS + 512] = np.tile(ms, (1, 8))
    p = np.arange(128)
    for tb in range(NB):
        for h in range(4):
            c32[:, C_ZETA + tb * 4 + h] = g[h] ** (T - 1 - (tb * 128 + p))
        c32[:, C_VAL + tb] = ((tb * 128 + p) >= LPAD)
    c16 = np.zeros((128, C_W16), np.float32)
    ii = np.arange(NB * 128)
    for h in range(4):
        dl = ii[None, :] - p[:, None]
        c16[:, C_D + h * NB * 128: C_D + (h + 1) * NB * 128] = np.where(dl >= 0, g[h] ** np.maximum(dl, 0), 0.0)
        c16[:, C_XI + h * T: C_XI + (h + 1) * T] = g[h] ** (np.arange(T) + 1.0)
    half = 128
    inv_freq = (10000.0 ** (-np.arange(half, dtype=np.float32) / half)).astype(np.float32)
    pos = (np.arange(NT * T) - LPAD).astype(np.float32)
    ang = (pos[:, None] * inv_freq[None, :]).astype(np.float32)
    cs = np.cos(ang.astype(np.float64)).T
    sn = np.sin(ang.astype(np.float64)).T
    tab = np.zeros((NT, 128, 4, T), np.float32)
    for n in range(NT):
        sl = slice(n * T, (n + 1) * T)
        tab[n, :, 0] = cs[:, sl]
        tab[n, :, 1] = sn[:, sl]
        tab[n, :, 2] = cs[:, sl] / 16.0
        tab[n, :, 3] = sn[:, sl] / 16.0
    import ml_dtypes
    return c32, c16.astype(ml_dtypes.bfloat16), tab


NSLOT = 4
SLOT_B = 8192

IN_SPECS = [
    ("meta_tokens", [16, D]), ("ret_w_in", [1, D, 6144]), ("ret_w_out", [1, 2048, D]),
    ("dn_w_in", [1, D, 6160]), ("dn_w_out", [1, 2048, D]),
    ("ffn_w_gate", [2, D, FF]), ("ffn_w_up", [2, D, FF]), ("ffn_w_down", [2, FF, D]),
]
R_FIN, R_GNW, R_DNW, R_ALOG, R_DTB = 0, 1024, 1536, 1792, 1800
R_W = 1808
P_NCOL, P_CONV = 0, 32
P_W = 32 + 128


OFFS = {}


class Arena:
    def __init__(self):
        self.off = 0

    def alloc(self, nbytes, align=64):
        self.off = (self.off + align - 1) // align * align
        o = self.off
        self.off += nbytes
        return o


def build_program(NT, n_layers=2, dbg=None, stage=99):
    nc = bass.Bass("TRN2", target_bir_lowering=False)
    SEQ = (NT - 1) * T
    dram = {}
    x = nc.dram_tensor("x", [SEQ, D], F32, kind="ExternalInput").ap()
    for name, shp in IN_SPECS:
        dram[name] = nc.dram_tensor(name, shp, F32, kind="ExternalInput").ap()
    d_c32 = nc.dram_tensor("c32", [128, C_W32], F32, kind="ExternalInput").ap()
    d_c16 = nc.dram_tensor("c16", [128, C_W16], BF16, kind="ExternalInput").ap()
    d_tab = nc.dram_tensor("tab", [NT, 128, 4, T], F32, kind="ExternalInput").ap()
    d_rp = nc.dram_tensor("rpack", [1, R_W], F32, kind="ExternalInput").ap()
    d_pp = nc.dram_tensor("ppack", [128, P_W], F32, kind="ExternalInput").ap()
    y = nc.dram_tensor("y", [SEQ, D], F32, kind="ExternalOutput").ap()
    dbg_out = {}
    if dbg:
        for nm, shp in dbg.items():
            dbg_out[nm] = nc.dram_tensor("dbg_" + nm, shp, F32, kind="ExternalOutput").ap()
    s_rin = nc.dram_tensor("s_rin", [D, 6144], BF16, kind="Internal").ap()
    s_din = nc.dram_tensor("s_din", [D, 6160], BF16, kind="Internal").ap()
    s_rout = nc.dram_tensor("s_rout", [128, 8, 16, 128], BF16, kind="Internal").ap()
    s_dout = nc.dram_tensor("s_dout", [128, 8, 16, 128], BF16, kind="Internal").ap()
    s_g = [nc.dram_tensor("s_g%d" % l, [D, FF], BF16, kind="Internal").ap() for l in range(2)]
    s_u = [nc.dram_tensor("s_u%d" % l, [D, FF], BF16, kind="Internal").ap() for l in range(2)]
    s_dw = [nc.dram_tensor("s_dw%d" % l, [128, 8, NHC, 128], BF16, kind="Internal").ap() for l in range(2)]

    with ExitStack() as st:
        A = Arena()
        o_c32 = A.alloc(C_W32 * 4)
        o_c16 = A.alloc(C_W16 * 2)
        o_idb = A.alloc(256)
        o_oneb = A.alloc(256)
        o_rp = A.alloc(R_W * 4)
        o_pp = A.alloc(P_W * 4)
        o_nexpa = A.alloc(32)
        o_wba = A.alloc(8 * 16 * 2)
        o_hT = A.alloc(8 * T * 4)
        o_hnT = A.alloc(8 * T * 2)
        o_tab = A.alloc(4 * T * 4)
        o_ring = A.alloc(NSLOT * SLOT_B)
        o_A = A.alloc(16 * T * 2)
        o_B = A.alloc(NB * 2048 * 2)
        assert o_B == o_A + 16 * T * 2 and NB * 2048 * 2 >= (NHC - 16) * T * 2
        o_C = A.alloc(NB * 2048 * 2)
        o_X = A.alloc(NB * 1024 * 4)
        o_E = A.alloc(8 * T * 2)
        o_F = A.alloc(NB * 1024 * 2)
        o_Sr = A.alloc(8 * 512 * 4)
        o_Srb = A.alloc(8 * 512 * 2)
        o_Sd = A.alloc(8 * 256 * 4)
        o_Sdb = A.alloc(8 * 256 * 2)
        o_Sh = A.alloc(2 * 1024)
        o_sq = A.alloc(2 * T * 2)
        o_t0 = A.alloc(T * 4)
        o_t1 = A.alloc(T * 4)
        o_junk = A.alloc(1024 * 2)
        o_small = A.alloc(512)
        o_ft = A.alloc(2 * T * 4)
        o_cin = A.alloc(4 * (T + 4) * 4)
        o_hist = A.alloc(32 * 4 * 4)
        o_cacc = A.alloc(4 * T * 4)
        o_gts = A.alloc(NB * 8 * 4 * 4)
        o_ch = o_X
        assert o_F + NB * 1024 * 2 - o_X >= 8 * 2048
        o_p1 = A.alloc(NCH * 3 * 1024 + NB * 2 * 2048)
        o_bcx = A.alloc(8 * 1024)
        o_ktok = A.alloc(2048)
        o_vb = A.alloc(2 * 4096)
        o_vnew = A.alloc(4096)
        o_nwT = A.alloc(1024)
        o_eg = A.alloc(NCH * 32 + 64)
        o_osq = o_ch + 6 * 2048
        NBYTES = (A.off + 63) // 64 * 64
        OFFS.update({k: v for k, v in locals().items() if k.startswith('o_')})
        print('arena', NBYTES)
        assert NBYTES <= 206 * 1024, NBYTES

        sb_bf = st.enter_context(nc.sbuf_tensor("arena", [128, NBYTES // 2], BF16))
        sb = Mem({BF16: sb_bf, F32: sb_bf.bitcast(F32)}, NBYTES, 512)
        ps_f = st.enter_context(nc.psum_tensor("psum", [128, 4096], F32))
        pm = Mem({F32: ps_f, BF16: ps_f.bitcast(BF16)}, 16384, 2048)
        S = Sched(nc, st)

        ps_ptr = [0]

        def ps(nbytes, dt, shape, p0=0, p1=128):
            al = 2048 if nbytes <= 2048 else (8192 if nbytes > 4096 else 4096)
            o = (ps_ptr[0] + al - 1) // al * al
            if o + nbytes > 16384:
                o = 0
            ps_ptr[0] = o + nbytes
            return pm.view(o, dt, shape, p0, p1)

        c32 = sb.view(o_c32, F32, [C_W32])
        c16 = sb.view(o_c16, BF16, [C_W16])
        identf = c32[:, C_ID:C_ID + 128]
        onesf = c32[:, C_ONE:C_ONE + 128]
        identb = sb.view(o_idb, BF16, [128])
        onesb = sb.view(o_oneb, BF16, [128])
        rp = sb.view(o_rp, F32, [R_W])
        pp = sb.view(o_pp, F32, [P_W])
        nexpa = sb.view(o_nexpa, F32, [8])
        wba = sb.view(o_wba, BF16, [8, 16])
        hT = sb.view(o_hT, F32, [8, T])
        hnT = sb.view(o_hnT, BF16, [8, T])
        tab = sb.view(o_tab, F32, [4, T])
        xin = sb.view(o_X, F32, [NB, 1024])

        def cast_plain(dst, src):
            K, N = src.shape
            for r0 in range(0, K, 128):
                c0 = 0
                while c0 < N:
                    w_ = min(2048, N - c0)
                    cw_ = 512 if w_ % 512 == 0 else 16
                    S.dma("pool", "cast", [(dst[r0:r0 + 128, c0:c0 + w_].rearrange("p (a b) -> p a b", b=cw_),
                                            src[r0:r0 + 128, c0:c0 + w_].rearrange("p (a b) -> p a b", b=cw_))])
                    c0 += w_

        def cast_oc(dst, src, KC):
            sv = src.rearrange("(kc p) (oc c) -> oc p kc c", p=128, c=128)
            for oc in range(8):
                S.dma("pool", "cast", [(dst[:, oc, :, :], sv[oc])])

        cast_plain(s_rin, dram["ret_w_in"][0])
        cast_oc(s_rout, dram["ret_w_out"][0], 16)
        for l in range(2):
            cast_plain(s_g[l], dram["ffn_w_gate"][l])
            cast_plain(s_u[l], dram["ffn_w_up"][l])
            cast_oc(s_dw[l], dram["ffn_w_down"][l], NHC)
            if l == 0:
                cast_plain(s_din, dram["dn_w_in"][0])
                cast_oc(s_dout, dram["dn_w_out"][0], 16)

        if stage < 1:
            S.wait_dma_all("sp", "cast")
            return nc
        S.dma("sp", "const", [(c32, d_c32), (c16, d_c16), (pp, d_pp),
                              (rp, d_rp.broadcast_to([128, R_W]))], writes=[c32, c16, pp, rp])
        if stage < 1.1:
            S.wait_dma_all("act", "const")
            return nc
        S.copy("dve", identb, identf)
        S.copy("dve", onesb, onesf)
        if stage < 1.2:
            S._wait(S.E["act"], "dve", 2)
            return nc
        S.wait_dma_all("sp", "cast")
        S.dma("sp", "const2", [(wba, s_din.rearrange("(kc p) n -> p kc n", p=128)[:, :, 6144:6160])], writes=[wba])
        if stage < 1.5:
            S.wait_dma_all("act", "const2")
            return nc
        S.act(nexpa, rp[:, R_ALOG:R_ALOG + 8], AF.Exp)
        S.ts("dve", nexpa, nexpa, -1.0, ALU.mult)
        if stage < 2:
            S.wait_dma_all("act", "const")
            S.wait_dma_all("act", "const2")
            return nc
        Sr = sb.view(o_Sr, F32, [8, 512])
        Srb = sb.view(o_Srb, BF16, [8, 512])
        Sd = sb.view(o_Sd, F32, [8, 256])
        Sdb = sb.view(o_Sdb, BF16, [8, 256])
        for v_ in (Sr, Srb, Sd, Sdb):
            S.memset("pool", v_, 0.0)

        def wlist():
            L = []
            rin = s_rin.rearrange("(kc p) n -> p kc n", p=128)
            for g in range(12):
                L.append(("rin%d" % g, rin[:, :, g * 512:(g + 1) * 512], [8, 512]))
            for g in range(4):
                L.append(("rout%d" % g, s_rout[:, 2 * g:2 * g + 2, :, :], [2, 16, 128]))

            def ffn(l):
                gg = s_g[l].rearrange("(kc p) n -> p kc n", p=128)
                uu = s_u[l].rearrange("(kc p) n -> p kc n", p=128)
                for g in range(6):
                    ncol = min(512, FF - g * 512)
                    L.append(("g%d_%d" % (l, g), gg[:, :, g * 512:g * 512 + ncol], [8, ncol]))
                    L.append(("u%d_%d" % (l, g), uu[:, :, g * 512:g * 512 + ncol], [8, ncol]))
                for oc in range(8):
                    L.append(("dw%d_%d" % (l, oc), s_dw[l][:, oc, :, :], [NHC, 128]))
            ffn(0)
            if n_layers > 1:
                din = s_din.rearrange("(kc p) n -> p kc n", p=128)
                for g in range(12):
                    L.append(("din%d" % g, din[:, :, g * 512:(g + 1) * 512], [8, 512]))
                for g in range(4):
                    L.append(("dout%d" % g, s_dout[:, 2 * g:2 * g + 2, :, :], [2, 16, 128]))
                ffn(1)
            return L

        WL = wlist()
        NW = len(WL)
        wstate = {"issued": 0, "used": 0}
        total_w = NW * NT

        def w_issue():
            i = wstate["issued"]
            name, src, shp = WL[i % NW]
            slot = i % NSLOT
            dst = sb.view(o_ring + slot * SLOT_B, BF16, shp)
            S.dma("sp", "w%d" % slot, [(dst, src)], writes=[dst])
            wstate["issued"] += 1

        def wnext(name):
            i = wstate["used"]
            nm, src, shp = WL[i % NW]
            assert nm == name, (nm, name)
            while wstate["issued"] < min(total_w, i + NSLOT - 1):
                w_issue()
            wstate["used"] += 1
            return sb.view(o_ring + (i % NSLOT) * SLOT_B, BF16, shp)

        sqb = [sb.view(o_sq + k * T * 2, BF16, [T]) for k in range(2)]
        t0 = sb.view(o_t0, F32, [T])
        t1 = sb.view(o_t1, F32, [T])
        junk = sb.view(o_junk, BF16, [1024])
        small = sb.view(o_small, F32, [64])
        flip = [0]

        def evac(out, in_):
            flip[0] ^= 1
            S.copy("act" if flip[0] else "dve", out, in_)

        def rstd_from(ssq_v, out_v, mean_scale, p0=0, p1=128):
            S.act(out_v, ssq_v, AF.Ln, bias=epsc[p0:p1], scale=mean_scale)
            S.act(out_v, out_v, AF.Exp, scale=-0.5)

        epsc = sb.view(o_small + 256, F32, [1])
        S.memset("pool", epsc, EPS)
        onec = sb.view(o_small + 260, F32, [1])
        S.memset("pool", onec, 1.0)

        def rmsnorm(col0):
            ssq = ps(T * 4, F32, [T])
            for kc in range(8):
                S.act(sqb[kc % 2], hT[:, kc, :], AF.Square)
                S.mm(ssq, onesb, sqb[kc % 2], start=(kc == 0), stop=(kc == 7))
            rstd_from(ssq, t0, 1.0 / D)
            for kc in range(8):
                S.stt(hnT[:, kc, :], hT[:, kc, :], pp[:, P_NCOL + col0 + kc:P_NCOL + col0 + kc + 1], t0, ALU.mult, ALU.mult)

        def proj_fm(wg, ci, out_ps):
            for kc in range(8):
                S.mm(out_ps, wg[:, kc, ci * 128:(ci + 1) * 128], hnT[:, kc, :], start=(kc == 0), stop=(kc == 7))

        def proj_tm(wg, tb, out_ps, ncol=512):
            for kc in range(8):
                S.mm(out_ps, hnT[:, kc, tb * 128:(tb + 1) * 128], wg[:, kc, 0:ncol], start=(kc == 0), stop=(kc == 7))

        sgw = sb.view(o_C, BF16, [NB, 2048])
        ogT = sb.view(o_A, BF16, [16, T])

        def mixer_out(wname):
            for fc in range(16):
                tp = ps(T * 2, BF16, [T])
                for tb in range(NB):
                    S.tr(tp[:, tb * 128:(tb + 1) * 128], sgw[:, tb, fc * 128:(fc + 1) * 128], identb)
                evac(ogT[:, fc, :], tp)
            for g in range(4):
                wg = wnext("%s%d" % (wname, g))
                for o2 in range(2):
                    oc = 2 * g + o2
                    mp = ps(T * 4, F32, [T])
                    for fc in range(16):
                        S.mm(mp, wg[:, o2, fc, :], ogT[:, fc, :], start=(fc == 0), stop=(fc == 15))
                    S.tt("dve", hT[:, oc, :], mp, hT[:, oc, :], ALU.add)

        ffa = sb.view(o_A, BF16, [NHC, T])
        fts = [sb.view(o_ft + k * T * 4, F32, [T]) for k in range(2)]

        def ffn(l):
            rmsnorm(16 + l * 8)
            for g in range(6):
                ncol = min(512, FF - g * 512)
                wgg = wnext("g%d_%d" % (l, g))
                wgu = wnext("u%d_%d" % (l, g))
                for ci in range(ncol // 128):
                    hc = g * 4 + ci
                    gp = ps(T * 4, F32, [T])
                    proj_fm(wgg, ci, gp)
                    up = ps(T * 4, F32, [T])
                    proj_fm(wgu, ci, up)
                    ft = fts[hc % 2]
                    S.act(ft, gp, AF.Silu)
                    S.tt("dve", ffa[:, hc, :], up, ft, ALU.mult)
            for oc in range(8):
                wg = wnext("dw%d_%d" % (l, oc))
                mp = ps(T * 4, F32, [T])
                for hc in range(NHC):
                    S.mm(mp, wg[:, hc, :], ffa[:, hc, :], start=(hc == 0), stop=(hc == NHC - 1))
                S.tt("dve", hT[:, oc, :], mp, hT[:, oc, :], ALU.add)

        gam = _gammas()
        qkT = sb.view(o_A, BF16, [16, T])
        vtok = sb.view(o_B, BF16, [NB, 2048])
        qxiT = sb.view(o_E, BF16, [8, T])
        kz = sb.view(o_F, BF16, [NB, 1024])
        rtmp = [sb.view(o_X + k * T * 4, F32, [T]) for k in range(4)]
        Shb = [sb.view(o_Sh + k * 1024, BF16, [512]) for k in range(2)]
        gnw_row = rp[:, R_GNW:R_GNW + 512]

        def retention():
            rmsnorm(0)
            for g in range(4):
                wg = wnext("rin%d" % g)
                isk = g >= 2
                for hh in range(2):
                    h = (g % 2) * 2 + hh
                    p1 = ps(T * 4, F32, [T])
                    proj_fm(wg, hh * 2, p1)
                    p2 = ps(T * 4, F32, [T])
                    proj_fm(wg, hh * 2 + 1, p2)
                    cs = tab[:, 2 if isk else 0, :]
                    sn = tab[:, 3 if isk else 1, :]
                    dst = qkT[:, (8 if isk else 0) + h * 2, :]
                    dst2 = qkT[:, (8 if isk else 0) + h * 2 + 1, :]
                    S.tt("dve", rtmp[0], p1, cs, ALU.mult)
                    S.tt("dve", rtmp[1], p2, sn, ALU.mult)
                    S.tt("pool", dst, rtmp[0], rtmp[1], ALU.subtract)
                    S.tt("dve", rtmp[2], p1, sn, ALU.mult)
                    S.tt("dve", rtmp[3], p2, cs, ALU.mult)
                    S.tt("pool", dst2, rtmp[2], rtmp[3], ALU.add)
            for g in range(4):
                wg = wnext("rin%d" % (4 + g))
                for tb in range(NB):
                    vp = ps(2048, F32, [512])
                    proj_tm(wg, tb, vp)
                    evac(vtok[:, tb, g * 512:(g + 1) * 512], vp)
            for g in range(4):
                wg = wnext("rin%d" % (8 + g))
                for tb in range(NB):
                    gp = ps(2048, F32, [512])
                    proj_tm(wg, tb, gp)
                    dst = sgw[:, tb, g * 512:(g + 1) * 512]
                    S.act(dst, gp, AF.Silu)
                    S.tt("pool", dst, dst, gnw_row, ALU.mult)
            for c8 in range(8):
                h = c8 // 2
                S.tt("pool", qxiT[:, c8, :], qkT[:, c8, :], c16[:, C_XI + h * T:C_XI + (h + 1) * T], ALU.mult)
            for tb in range(NB):
                for h in range(4):
                    kp = ps(512, BF16, [256])
                    for half in range(2):
                        S.tr(kp[:, half * 128:(half + 1) * 128], qkT[:, 8 + h * 2 + half, tb * 128:(tb + 1) * 128], identb)
                    S.act(kz[:, tb, h * 256:(h + 1) * 256], kp, AF.Copy,
                          scale=c32[:, C_ZETA + tb * 4 + h:C_ZETA + tb * 4 + h + 1])
            for h in range(4):
                Sh = Shb[h % 2]
                offs = []
                o = 0
                for jb in range(NB):
                    N = T - jb * 128
                    sp_ = ps(N * 4, F32, [N])
                    for half in range(2):
                        S.mm(sp_, qkT[:, 8 + h * 2 + half, jb * 128:(jb + 1) * 128], qkT[:, h * 2 + half, jb * 128:T],
                             start=(half == 0), stop=(half == 1))
                    S.tt("dve", Sh[:, o:o + N], sp_, c16[:, C_D + h * NB * 128:C_D + h * NB * 128 + N], ALU.mult)
                    offs.append(o)
                    o += N
                for ib in range(NB):
                    op_ = ps(2048, F32, [512])
                    first = True
                    for jb in range(ib + 1):
                        S.mm(op_, Sh[:, offs[jb] + (ib - jb) * 128: offs[jb] + (ib - jb + 1) * 128],
                             vtok[:, jb, h * 512:(h + 1) * 512], start=first, stop=False)
                        first = False
                    for half in range(2):
                        S.mm(op_, qxiT[:, h * 2 + half, ib * 128:(ib + 1) * 128], Srb[:, h * 2 + half, :],
                             start=False, stop=(half == 1))
                    ssq = small[:, h * 4 + ib:h * 4 + ib + 1]
                    S.act(junk[:, 0:512], op_, AF.Square, accum_out=ssq)
                    rstd_from(ssq, ssq, 1.0 / 512)
                    dst = sgw[:, ib, h * 512:(h + 1) * 512]
                    S.stt(dst, op_, ssq, dst, ALU.mult, ALU.mult)
                for half in range(2):
                    dp = ps(2048, F32, [512])
                    for jb in range(NB):
                        S.mm(dp, kz[:, jb, h * 256 + half * 128:h * 256 + (half + 1) * 128],
                             vtok[:, jb, h * 512:(h + 1) * 512], start=(jb == 0), stop=(jb == NB - 1))
                    sv = Sr[:, h * 2 + half, :]
                    S.stt(sv, sv, float(gam[h] ** T), dp, ALU.mult, ALU.add)
                    S.copy("pool", Srb[:, h * 2 + half, :], sv)
            mixer_out("rout")


        qknT = sb.view(o_A, BF16, [16, T])
        vT = sb.view(o_B, BF16, [16, T])
        cins = [sb.view(o_cin + k * (T + 4) * 4, F32, [T + 4]) for k in range(4)]
        hist = sb.view(o_hist, F32, [32, 4])
        caccs = [sb.view(o_cacc + k * T * 4, F32, [T]) for k in range(4)]
        gts = sb.view(o_gts, F32, [4, NB, 8])
        chv = [sb.view(o_ch + k * 2048, F32, [8, 64]) for k in range(8)]
        NTp = [sb.view(o_p1 + c * 3072, BF16, [8, 64]) for c in range(NCH)]
        qkp = [sb.view(o_p1 + c * 3072 + 1024, BF16, [8, 64]) for c in range(NCH)]
        qgT = [sb.view(o_p1 + c * 3072 + 2048, BF16, [8, 64]) for c in range(NCH)]
        kbg = [sb.view(o_p1 + NCH * 3072 + tb * 4096, BF16, [8, 128]) for tb in range(NB)]
        kdec = [sb.view(o_p1 + NCH * 3072 + tb * 4096 + 2048, BF16, [8, 128]) for tb in range(NB)]
        bcx = [sb.view(o_bcx + k * 1024, BF16, [8, 64]) for k in range(8)]
        ktok = sb.view(o_ktok, BF16, [8, 128])
        vbf = [sb.view(o_vb + k * 4096, BF16, [8, 256]) for k in range(2)]
        vnew = sb.view(o_vnew, BF16, [8, 256])
        nwT = sb.view(o_nwT, BF16, [8, 64])
        egl = sb.view(o_eg, F32, [NCH, 8])
        sm8 = sb.view(o_eg + NCH * 32, F32, [16])
        osq = sb.view(o_osq, BF16, [2048])
        lnqc = sb.view(o_small + 264, F32, [1])
        S.memset("pool", lnqc, float(np.log(128.0 ** -0.5)))
        S.memset("pool", hist, 0.0)
        S.memset("pool", vnew, 0.0)
        for c in range(NCH):
            S.memset("pool", NTp[c], 0.0)
            S.memset("pool", qkp[c], 0.0)
        dnw_row = rp[:, R_DNW:R_DNW + 256]
        TLE = c32[:, C_TLE:C_TLE + 64]
        TGT = c32[:, C_TGT:C_TGT + 64]

        def deltanet(n):
            rmsnorm(8)
            for g in range(8):
                wg = wnext("din%d" % g)
                for cp in range(2):
                    ccs = [g * 4 + cp * 2, g * 4 + cp * 2 + 1]
                    pps = []
                    for cc in ccs:
                        pp_ = ps(T * 4, F32, [T])
                        proj_fm(wg, cc - g * 4, pp_)
                        pps.append(pp_)
                    for cc, pp_ in zip(ccs, pps):
                        cin = cins[cc % 4]
                        S.copy("pool", cin[:, 0:3], hist[:, cc, 0:3])
                        S.act(cin[:, 3:3 + T], pp_, AF.Copy)
                        if n == 0:
                            S.memset("pool", cin[:, 3:3 + LPAD], 0.0)
                        S.copy("pool", hist[:, cc, 0:3], cin[:, T:T + 3])
                    cw = lambda cc, t: pp[:, P_CONV + cc * 4 + t:P_CONV + cc * 4 + t + 1]
                    for cc in ccs:
                        S.ts("dve", caccs[cc % 4], cins[cc % 4][:, 3:3 + T], cw(cc, 3), ALU.mult)
                    for t in (2, 1, 0):
                        for cc in ccs:
                            S.stt(caccs[cc % 4], cins[cc % 4][:, t:t + T], cw(cc, t), caccs[cc % 4], ALU.mult, ALU.add)
                    for cc in ccs:
                        if cc < 16:
                            S.act(qknT[:, cc, :], caccs[cc % 4], AF.Silu)
                        else:
                            S.act(vT[:, cc - 16, :], caccs[cc % 4], AF.Silu)
                if g == 1 or g == 3:
                    base = 0 if g == 1 else 8
                    blk = qknT[:, base:base + 8, :]
                    sqbig = sb.view(o_osq, BF16, [8, T])
                    S.act(sqbig, blk, AF.Square)
                    ssq8 = ps(8 * T * 4, F32, [8, T])
                    for h in range(8):
                        S.mm(ssq8[:, h, :], onesb, sqbig[:, h, :])
                    rst = sb.view(o_ch, F32, [8, T])
                    S.act(rst, ssq8, AF.Ln, bias=epsc, scale=1.0)
                    if base == 0:
                        S.act(rst, rst, AF.Exp, bias=lnqc, scale=-0.5)
                    else:
                        S.act(rst, rst, AF.Exp, scale=-0.5)
                    S.tt("dve", blk, blk, rst, ALU.mult)
            for g in range(4):
                wg = wnext("din%d" % (8 + g))
                for tb in range(NB):
                    gp = ps(2048, F32, [512])
                    proj_tm(wg, tb, gp)
                    dst = sgw[:, tb, g * 512:(g + 1) * 512]
                    S.act(dst, gp, AF.Silu)
                    d3 = dst.rr("p (h v) -> p h v", h=2)
                    S.tt("pool", d3, d3, dnw_row.us(1).bc([128, 2, 256]), ALU.mult)
            for tb in range(NB):
                bp = ps(64, F32, [16])
                for kc in range(8):
                    S.mm(bp, hnT[:, kc, tb * 128:(tb + 1) * 128], wba[:, kc, :], start=(kc == 0), stop=(kc == 7))
                S.act(gts[:, 1, tb, :], bp[:, 0:8], AF.Sigmoid)
                S.tt("dve", gts[:, 3, tb, :], bp[:, 8:16], rp[:, R_DTB:R_DTB + 8], ALU.add)
                S.act(gts[:, 3, tb, :], gts[:, 3, tb, :], AF.Exp)
                S.act(gts[:, 3, tb, :], gts[:, 3, tb, :], AF.Ln, bias=onec, scale=1.0)
                S.tt("dve", gts[:, 0, tb, :], gts[:, 3, tb, :], nexpa, ALU.mult)
                if n == 0:
                    vcol = c32[:, C_VAL + tb:C_VAL + tb + 1]
                    S.ts("dve", gts[:, 0, tb, :], gts[:, 0, tb, :], vcol, ALU.mult)
                    S.ts("dve", gts[:, 1, tb, :], gts[:, 1, tb, :], vcol, ALU.mult)
                S.ts("dve", gts[:, 2, tb, :], gts[:, 1, tb, :], -1.0, ALU.mult)
            for tb in range(NB):
                ktp = ps(2048, BF16, [8, 128])
                for h in range(8):
                    S.tr(ktp[:, h, :], qknT[:, 8 + h, tb * 128:(tb + 1) * 128], identb)
                S.copy("act", ktok, ktp)
                for half in range(2):
                    c = tb * 2 + half
                    P0, P1 = half * 64, half * 64 + 64
                    tok = slice(c * 64, (c + 1) * 64)
                    g_c = gts[P0:P1, 0, tb, :]
                    beta_c = gts[P0:P1, 1, tb, :]
                    nbeta_c = gts[P0:P1, 2, tb, :]
                    gtri1, decT, decS, tmul, egrow, gtri2 = chv[0][P0:P1], chv[1][P0:P1], chv[2][P0:P1], chv[3][P0:P1], chv[4], chv[5][P0:P1]
                    S.tt("pool", gtri1, g_c.us(2).bc([64, 8, 64]), TLE[P0:P1].us(1).bc([64, 8, 64]), ALU.mult)
                    S.tt("pool", gtri2, g_c.us(2).bc([64, 8, 64]), TGT[P0:P1].us(1).bc([64, 8, 64]), ALU.mult)
                    g1f = gtri1.rr("p h i -> p (h i)")
                    g2f = gtri2.rr("p h i -> p (h i)")
                    id64 = identf[P0:P1, P0:P1]
                    gr_ps = ps(2048, F32, [512])
                    S.mm(gr_ps, onesf[P0:P1, :], g1f)
                    dT_ps = ps(2048, F32, [512], P0, P1)
                    S.mm(dT_ps, TGT[P0:P1], g1f, start=True, stop=False)
                    S.mm(dT_ps, id64, c32[P0:P1, C_MT:C_MT + 512], start=False, stop=True)
                    dS_ps = ps(2048, F32, [512], P0, P1)
                    S.mm(dS_ps, TLE[P0:P1], g2f, start=True, stop=False)
                    S.mm(dS_ps, id64, c32[P0:P1, C_MS:C_MS + 512], start=False, stop=True)
                    gc_ps = ps(32, F32, [8], P0, P1)
                    S.mm(gc_ps, TLE[P0:P1], g_c)
                    S.act(decT.rr("p h i -> p (h i)"), dT_ps, AF.Exp)
                    S.act(decS.rr("p h i -> p (h i)"), dS_ps, AF.Exp)
                    S.act(egrow.rr("p h i -> p (h i)"), gr_ps, AF.Exp)
                    gcol = sm8[P0:P1, 0:8]
                    dcol = sm8[P0:P1, 8:16]
                    S.copy("act", gcol, gc_ps)
                    S.tt("dve", dcol, gr_ps[P0:P1].rr("p (h i) -> p h i", h=8)[:, :, 63], gcol, ALU.subtract)
                    S.act(dcol, dcol, AF.Exp)
                    S.act(gcol, gcol, AF.Exp)
                    S.tt("dve", gcol, gcol, beta_c, ALU.mult)
                    S.copy("pool", egl[:, c, :], egrow[:, :, 63])
                    kk_ps = ps(2048, F32, [8, 64], P0, P1)
                    qk_ps = ps(2048, F32, [8, 64], P0, P1)
                    for h in range(8):
                        S.mm(kk_ps[:, h, :], qknT[:, 8 + h, tok], qknT[:, 8 + h, tok])
                    for h in range(8):
                        S.mm(qk_ps[:, h, :], qknT[:, 8 + h, tok], qknT[:, h, tok])
                    S.tt("pool", tmul, decS, nbeta_c.us(2).bc([64, 8, 64]), ALU.mult)
                    B, C, X = bcx[0][P0:P1], bcx[1][P0:P1], bcx[2][P0:P1]
                    S.tt("dve", B, kk_ps, tmul, ALU.mult)
                    S.tt("dve", qkp[c][P0:P1], qk_ps, decT, ALU.mult)
                    ct_ps = ps(1024, BF16, [8, 64], P0, P1)
                    idb64 = identb[P0:P1, P0:P1]
                    for h in range(8):
                        S.tr(ct_ps[:, h, :], B[:, h, :], idb64)
                    S.copy("act", C, ct_ps)
                    S.tt("dve", X, ct_ps, id64.us(1).bc([64, 8, 64]), ALU.add)
                    for k in range(1, 6):
                        Bn, Cn, Xn = bcx[3 * (k % 2)][P0:P1], bcx[3 * (k % 2) + 1][P0:P1], bcx[3 * (k % 2) + 2][P0:P1]
                        if k == 5:
                            Xn = NTp[c][P0:P1]
                        b_ps = ps(2048, F32, [8, 64], P0, P1)
                        for h in range(8):
                            S.mm(b_ps[:, h, :], C[:, h, :], B[:, h, :])
                        if k < 5:
                            c_ps = ps(2048, F32, [8, 64], P0, P1)
                            for h in range(8):
                                S.mm(c_ps[:, h, :], B[:, h, :], C[:, h, :])
                        S.copy("act", Bn, b_ps)
                        if k < 5:
                            S.copy("dve", Cn, c_ps)
                        x_ps = ps(2048, F32, [8, 64], P0, P1)
                        for h in range(8):
                            S.mm(x_ps[:, h, :], Bn[:, h, :], X[:, h, :])
                        S.tt("dve", Xn, x_ps, X, ALU.add)
                        B, C, X = Bn, Cn, Xn
                    S.tt("pool", kbg[tb][P0:P1], ktok[P0:P1], gcol.us(2).bc([64, 8, 128]), ALU.mult)
                    S.tt("pool", kdec[tb][P0:P1], ktok[P0:P1], dcol.us(2).bc([64, 8, 128]), ALU.mult)
                    S.tt("pool", qgT[c], qknT[:, 0:8, tok], egrow, ALU.mult)
            for tb in range(NB):
                vtp = ps(4096, BF16, [16, 128])
                for vc in range(16):
                    S.tr(vtp[:, vc, :], vT[:, vc, tb * 128:(tb + 1) * 128], identb)
                vb = vbf[tb % 2]
                S.tt("dve", vb, vtp.rr("p (h a) c -> p h (a c)", h=8), gts[:, 1, tb, :].us(2).bc([128, 8, 256]), ALU.mult)
                for half in range(2):
                    c = tb * 2 + half
                    P0, P1 = half * 64, half * 64 + 64
                    w_ps = ps(2048, F32, [8, 64])
                    for h in range(8):
                        S.mm(w_ps[:, h, :], kbg[tb][P0:P1, h, :], NTp[c][P0:P1, h, :])
                    S.act(nwT, w_ps, AF.Copy, scale=-1.0)
                    vn_ps = ps(8192, F32, [8, 256], P0, P1)
                    for h in range(8):
                        S.mm(vn_ps[:, h, :], NTp[c][:, h, :], vb[:, h, :], start=True, stop=False)
                        S.mm(vn_ps[:, h, :], nwT[:, h, :], Sdb[:, h, :], start=False, stop=True)
                    S.copy("act", vnew[P0:P1, 0:4, :], vn_ps[:, 0:4, :])
                    S.copy("dve", vnew[P0:P1, 4:8, :], vn_ps[:, 4:8, :])
                    o_ps = ps(8192, F32, [8, 256], P0, P1)
                    for h in range(8):
                        S.mm(o_ps[:, h, :], qkp[c][:, h, :], vnew[:, h, :], start=True, stop=False)
                        S.mm(o_ps[:, h, :], qgT[c][:, h, :], Sdb[:, h, :], start=False, stop=True)
                    S.act(osq[P0:P1], o_ps.rr("p h v -> p (h v)"), AF.Square)
                    ss8 = sm8[P0:P1, 0:8]
                    S.reduce_add(ss8, osq[P0:P1].rr("p (h v) -> p h v", h=8))
                    rstd_from(ss8, ss8, 1.0 / 256, P0, P1)
                    for h in range(8):
                        dst = sgw[P0:P1, tb, h * 256:(h + 1) * 256]
                        S.stt(dst, o_ps[:, h, :], ss8[:, h:h + 1], dst, ALU.mult, ALU.mult)
                    d_ps = ps(8192, F32, [8, 256])
                    for h in range(8):
                        S.mm(d_ps[:, h, :], kdec[tb][P0:P1, h, :], vnew[P0:P1, h, :])
                    for h in range(8):
                        S.stt(Sd[:, h, :], Sd[:, h, :], egl[:, c, h:h + 1], d_ps[:, h, :], ALU.mult, ALU.add)
                    S.copy("pool", Sdb, Sd)
            mixer_out("dout")

        finw_row = rp[:, R_FIN:R_FIN + 1024]
        for n in range(NT):
            if n == 0:
                S.memset("pool", xin, 0.0)
                S.dma("sp", "xin", [(xin[112:128, NB - 1, :], dram["meta_tokens"])], writes=[xin])
            else:
                S.dma("sp", "xin", [(xin, x[(n - 1) * T:n * T, :].rearrange("(tb p) d -> p tb d", p=128))], writes=[xin])
            S.dma("sp", "tab", [(tab, d_tab[n])], writes=[tab])
            for kc in range(8):
                tp = ps(T * 4, F32, [T])
                for tb in range(NB):
                    S.tr(tp[:, tb * 128:(tb + 1) * 128], xin[:, tb, kc * 128:(kc + 1) * 128], identf)
                evac(hT[:, kc, :], tp)
            if stage < 3:
                S.dma("sp", "yout", [(y[0:128, :], hT.rr("p a b -> p (a b)")[:, 0:1024])], reads=[hT])
                break
            retention()
            if stage < 4:
                S.dma("sp", "yout", [(y[0:128, :], hT.rr("p a b -> p (a b)")[:, 0:1024])], reads=[hT])
                break
            ffn(0)
            if n_layers > 1:
                deltanet(n)
                ffn(1)
            if n >= 1:
                for tb in range(NB):
                    fp = ps(4096, F32, [1024])
                    for kc in range(8):
                        S.tr(fp[:, kc * 128:(kc + 1) * 128], hT[:, kc, tb * 128:(tb + 1) * 128], identf)
                    ssq = small[:, 32 + tb:33 + tb]
                    S.act(junk, fp, AF.Square, accum_out=ssq)
                    rstd_from(ssq, ssq, 1.0 / D)
                    S.stt(xin[:, tb, :], fp, ssq, finw_row, ALU.mult, ALU.mult)
                S.dma("sp", "yout", [(y[(n - 1) * T:n * T, :].rearrange("(tb p) d -> p tb d", p=128), xin)], reads=[xin])
        assert stage < 99 or wstate["used"] == total_w
        if "yout" in S.dsem:
            S.wait_dma_all("sp", "yout")
        for nm, ap_ in dbg_out.items():
            pass
        print("instructions:", S.n_ins, "arena bytes:", NBYTES)
    return nc


def make_in_map(inp, x_core, NT):
    c32, c16, tab = host_consts(NT)
    f = lambda a: np.ascontiguousarray(np.asarray(a, dtype=np.float32))
    m = {"x": f(x_core)}
    for name, shp in IN_SPECS:
        m[name] = f(inp[name]).reshape(shp)
    m["c32"] = c32
    m["c16"] = c16
    m["tab"] = tab
    rp = np.zeros((1, R_W), np.float32)
    rp[0, R_FIN:R_FIN + 1024] = f(inp["final_norm_w"])
    rp[0, R_GNW:R_GNW + 512] = f(inp["ret_gn_w"]).reshape(-1)
    rp[0, R_DNW:R_DNW + 256] = f(inp["dn_norm_w"]).reshape(-1)
    rp[0, R_ALOG:R_ALOG + 8] = f(inp["dn_a_log"]).reshape(-1)
    rp[0, R_DTB:R_DTB + 8] = f(inp["dn_dt_bias"]).reshape(-1)
    m["rpack"] = rp
    pp = np.zeros((128, P_W), np.float32)
    nv = [f(inp["mix_norm_w"])[0], f(inp["mix_norm_w"])[1], f(inp["ffn_norm_w"])[0], f(inp["ffn_norm_w"])[1]]
    for vi, v in enumerate(nv):
        pp[:, P_NCOL + vi * 8:P_NCOL + vi * 8 + 8] = v.reshape(8, 128).T
    cw = f(inp["dn_conv_w"])[0]
    pp[:, P_CONV:P_CONV + 128] = cw.reshape(4, 32, 128).transpose(2, 1, 0).reshape(128, 128)
    m["ppack"] = pp
    return m


_NC_CACHE = {}


def kernel(x, meta_tokens, mix_norm_w, ffn_norm_w, ret_w_in, ret_gn_w, ret_w_out,
           dn_w_in, dn_conv_w, dn_a_log, dn_dt_bias, dn_norm_w, dn_w_out,
           ffn_w_gate, ffn_w_up, ffn_w_down, final_norm_w):
    x = np.asarray(x)
    B, SEQ, _ = x.shape
    assert SEQ % T == 0
    NT = SEQ // T + 1
    inp = dict(meta_tokens=meta_tokens, mix_norm_w=mix_norm_w, ffn_norm_w=ffn_norm_w, ret_w_in=ret_w_in,
               ret_gn_w=ret_gn_w, ret_w_out=ret_w_out, dn_w_in=dn_w_in, dn_conv_w=dn_conv_w,
               dn_a_log=dn_a_log, dn_dt_bias=dn_dt_bias, dn_norm_w=dn_norm_w, dn_w_out=dn_w_out,
               ffn_w_gate=ffn_w_gate, ffn_w_up=ffn_w_up, ffn_w_down=ffn_w_down, final_norm_w=final_norm_w)
    inp = {k: np.asarray(v) for k, v in inp.items()}
    if NT not in _NC_CACHE:
        _NC_CACHE[NT] = build_program(NT, n_layers=2)
    nc = _NC_CACHE[NT]
    base = make_in_map(inp, x[0], NT)
    in_maps = []
    for b in range(B):
        m = dict(base)
        m["x"] = np.ascontiguousarray(x[b], dtype=np.float32)
        in_maps.append(m)
    res = run_bass_kernel_spmd(nc, in_maps, core_ids=list(range(B)))
    return np.stack([np.asarray(r["y"]) for r in res.results], axis=0).astype(np.float32)
```

```python
import numpy as np
from contextlib import ExitStack
import concourse.bass as bass
import concourse.mybir as mybir
from concourse.bass_utils import run_bass_kernel_spmd

F32 = mybir.dt.float32
BF16 = mybir.dt.bfloat16
AF = mybir.ActivationFunctionType
ALU = mybir.AluOpType

T = 256
NB = T // 128
NCH = T // 64
LPAD = 240
D = 1024
FF = 2816
NHC = FF // 128
EPS = 1e-6
NEG = -30000.0
SELF_SYNC = True


def _esz(dt):
    return 2 if dt == BF16 else 4


class Region:
    __slots__ = ("w", "r")

    def __init__(self):
        self.w = None
        self.r = {}


class Mem:
    def __init__(self, handles, nbytes, gran):
        self.h = handles
        self.gran = gran
        ng = (nbytes + gran - 1) // gran
        self.regs = [[Region() for _ in range(ng)] for _ in range(2)]

    def view(self, off, dt, shape, p0=0, p1=128):
        e = _esz(dt)
        assert off % e == 0
        n = int(np.prod(shape))
        ap = self.h[dt][p0:p1, off // e: off // e + n]
        if len(shape) == 2:
            ap = ap.rearrange("p (a b) -> p a b", a=shape[0])
        elif len(shape) == 3:
            ap = ap.rearrange("p (a b c) -> p a b c", a=shape[0], b=shape[1])
        return V(ap, self)


class V:
    __slots__ = ("ap", "mem")

    def __init__(self, ap, mem):
        self.ap = ap
        self.mem = mem

    def __getitem__(self, idx):
        return V(self.ap[idx], self.mem)

    def rr(self, pat, **kw):
        return V(self.ap.rearrange(pat, **kw), self.mem)

    def bc(self, shape):
        return V(self.ap.broadcast_to(list(shape)), self.mem)

    def us(self, axis):
        return V(self.ap.unsqueeze(axis), self.mem)

    def regions(self):
        ap = self.ap
        e = _esz(ap.dtype)
        pat = ap.ap
        rowlen = pat[0][0]
        col = ap.offset % rowlen if rowlen > 0 else ap.offset
        ext = 1
        for (s, c) in pat[1:]:
            ext += (c - 1) * abs(s)
        b0 = col * e
        b1 = (col + ext) * e
        g = self.mem.gran
        pstart = ap.offset // rowlen if rowlen > 0 else 0
        pend = pstart + pat[0][1]
        out = []
        if pstart < 64:
            out += self.mem.regs[0][b0 // g: (b1 - 1) // g + 1]
        if pend > 64:
            out += self.mem.regs[1][b0 // g: (b1 - 1) // g + 1]
        return out


def _ap(x):
    return x.ap if isinstance(x, V) else x


class EngState:
    def __init__(self, name, eng, sem):
        self.name, self.eng, self.sem = name, eng, sem
        self.count = 0
        self.seen = {}


class Sched:
    def __init__(self, nc, stack):
        self.nc = nc
        self.stack = stack
        self.E = {}
        for name, eng in (("pe", nc.tensor), ("act", nc.scalar), ("dve", nc.vector),
                          ("pool", nc.gpsimd), ("sp", nc.sync)):
            sem = stack.enter_context(nc.semaphore("s_" + name))
            self.E[name] = EngState(name, eng, sem)
        self.dsem = {}
        self.n_ins = 0

    def sem_of(self, key):
        if isinstance(key, tuple):
            return self.dsem[key[1]][0]
        return self.E[key].sem

    def _wait(self, E, key, val):
        if E.seen.get(key, 0) >= val:
            return
        if key == E.name:
            if E.name in ("pe", "sp") or not SELF_SYNC:
                return
        E.eng.wait_ge(self.sem_of(key), val)
        E.seen[key] = val

    def _deps(self, E, reads, writes):
        deps = {}
        for v in reads:
            for rg in v.regions():
                if rg.w is not None:
                    k, val = rg.w
                    if deps.get(k, 0) < val:
                        deps[k] = val
        for v in writes:
            for rg in v.regions():
                if rg.w is not None:
                    k, val = rg.w
                    if deps.get(k, 0) < val:
                        deps[k] = val
                for k, val in rg.r.items():
                    if deps.get(k, 0) < val:
                        deps[k] = val
        for k, val in deps.items():
            self._wait(E, k, val)

    def op(self, en, fn, reads=(), writes=()):
        E = self.E[en]
        reads = [v for v in reads if isinstance(v, V)]
        writes = [v for v in writes if isinstance(v, V)]
        self._deps(E, reads, writes)
        ins = fn(E.eng)
        E.count += 1
        ins.then_inc(E.sem, 1)
        self.n_ins += 1
        key = E.name
        for v in reads:
            for rg in v.regions():
                rg.r[key] = E.count
        for v in writes:
            for rg in v.regions():
                rg.w = (key, E.count)
                rg.r = {}
        return ins

    def dma(self, qn, semname, pairs, reads=(), writes=(), **kw):
        Q = self.E[qn]
        if semname not in self.dsem:
            self.dsem[semname] = [self.stack.enter_context(self.nc.semaphore("d_" + semname)), 0]
        ds = self.dsem[semname]
        reads = [v for v in reads if isinstance(v, V)]
        writes = [v for v in writes if isinstance(v, V)]
        self._deps(Q, reads, writes)
        for (o, i) in pairs:
            Q.eng.dma_start(out=_ap(o), in_=_ap(i), **kw).then_inc(ds[0], 16)
            ds[1] += 16
        key = ("d", semname)
        for v in reads:
            for rg in v.regions():
                rg.r[key] = ds[1]
        for v in writes:
            for rg in v.regions():
                rg.w = (key, ds[1])
                rg.r = {}

    def wait_dma_all(self, en, semname):
        E = self.E[en]
        ds = self.dsem[semname]
        self._wait(E, ("d", semname), ds[1])

    def mm(self, out, lhsT, rhs, start=True, stop=True):
        return self.op("pe", lambda e: e.matmul(_ap(out), lhsT=_ap(lhsT), rhs=_ap(rhs), start=start, stop=stop),
                       reads=[lhsT, rhs], writes=[out])

    def tr(self, out, in_, ident):
        return self.op("pe", lambda e: e.transpose(out=_ap(out), in_=_ap(in_), identity=_ap(ident)),
                       reads=[in_, ident], writes=[out])

    def act(self, out, in_, func, bias=None, scale=None, accum_out=None, en="act"):
        kw = {}
        rd = [in_]
        if bias is not None:
            kw["bias"] = _ap(bias) if isinstance(bias, V) else bias
            rd.append(bias)
        if scale is not None:
            kw["scale"] = _ap(scale) if isinstance(scale, V) else scale
            rd.append(scale)
        wr = [out]
        if accum_out is not None:
            kw["accum_out"] = _ap(accum_out)
            wr.append(accum_out)
        return self.op("act", lambda e: e.activation(out=_ap(out), in_=_ap(in_), func=func, **kw),
                       reads=rd, writes=wr)

    def tt(self, en, out, in0, in1, op):
        return self.op(en, lambda e: e.tensor_tensor(out=_ap(out), in0=_ap(in0), in1=_ap(in1), op=op),
                       reads=[in0, in1], writes=[out])

    def ts(self, en, out, in0, s1, op0, s2=None, op1=None):
        rd = [in0, s1, s2]
        a1 = _ap(s1) if isinstance(s1, V) else s1
        a2 = _ap(s2) if isinstance(s2, V) else s2
        if op1 is None:
            return self.op(en, lambda e: e.tensor_scalar(out=_ap(out), in0=_ap(in0), scalar1=a1, scalar2=None, op0=op0),
                           reads=rd, writes=[out])
        return self.op(en, lambda e: e.tensor_scalar(out=_ap(out), in0=_ap(in0), scalar1=a1, scalar2=a2, op0=op0, op1=op1),
                       reads=rd, writes=[out])

    def stt(self, out, in0, scalar, in1, op0, op1):
        sc = _ap(scalar) if isinstance(scalar, V) else scalar
        return self.op("dve", lambda e: e.scalar_tensor_tensor(out=_ap(out), in0=_ap(in0), scalar=sc, in1=_ap(in1),
                                                                op0=op0, op1=op1),
                       reads=[in0, scalar, in1], writes=[out])

    def copy(self, en, out, in_):
        if en == "act":
            return self.act(out, in_, AF.Copy)
        return self.op(en, lambda e: e.tensor_copy(out=_ap(out), in_=_ap(in_)), reads=[in_], writes=[out])

    def memset(self, en, out, val):
        return self.op(en, lambda e: e.memset(_ap(out), val), writes=[out])

    def reduce_add(self, out, in_):
        return self.op("dve", lambda e: e.tensor_reduce(out=_ap(out), in_=_ap(in_), axis=mybir.AxisListType.X, op=ALU.add),
                       reads=[in_], writes=[out])


RET_H, RET_DK, RET_DV = 4, 256, 512
DN_H, DN_DK, DN_DV = 8, 128, 256


def _gammas():
    return (1.0 - np.exp2(-5.0 - np.arange(RET_H, dtype=np.float64)))


C_ID, C_ONE, C_TLE, C_TGT = 0, 128, 256, 320
C_MT, C_MS = 384, 896
C_ZETA = 1408
C_VAL = C_ZETA + NB * 4
C_W32 = C_VAL + NB
C_D, C_XI = 0, 4 * NB * 128
C_W16 = C_XI + 4 * T


def host_consts(NT):
    g = _gammas()
    c32 = np.zeros((128, C_W32), np.float32)
    c32[:, C_ID:C_ID + 128] = np.eye(128)
    c32[:, C_ONE:C_ONE + 128] = 1.0
    k = np.arange(128) % 64
    i = np.arange(64)
    c32[:, C_TLE:C_TLE + 64] = (k[:, None] <= i[None, :])
    c32[:, C_TGT:C_TGT + 64] = (i[None, :] < k[:, None])
    mt = np.where(i[None, :] < k[:, None], NEG, 0.0)
    ms = np.where(i[None, :] >= k[:, None], NEG, 0.0)
    c32[:, C_MT:C_MT + 512] = np.tile(mt, (1, 8))
    c32[:, C_MS:C_MS + 512] = np.tile(ms, (1, 8))
    p = np.arange(128)
    for tb in range(NB):
        for h in range(4):
            c32[:, C_ZETA + tb * 4 + h] = g[h] ** (T - 1 - (tb * 128 + p))
        c32[:, C_VAL + tb] = ((tb * 128 + p) >= LPAD)
    c16 = np.zeros((128, C_W16), np.float32)
    ii = np.arange(NB * 128)
    for h in range(4):
        dl = ii[None, :] - p[:, None]
        c16[:, C_D + h * NB * 128: C_D + (h + 1) * NB * 128] = np.where(dl >= 0, g[h] ** np.maximum(dl, 0), 0.0)
        c16[:, C_XI + h * T: C_XI + (h + 1) * T] = g[h] ** (np.arange(T) + 1.0)
    half = 128
    inv_freq = (10000.0 ** (-np.arange(half, dtype=np.float32) / half)).astype(np.float32)
    pos = (np.arange(NT * T) - LPAD).astype(np.float32)
    ang = (pos[:, None] * inv_freq[None, :]).astype(np.float32)
    cs = np.cos(ang.astype(np.float64)).T
    sn = np.sin(ang.astype(np.float64)).T
    tab = np.zeros((NT, 128, 4, T), np.float32)
    for n in range(NT):
        sl = slice(n * T, (n + 1) * T)
        tab[n, :, 0] = cs[:, sl]
        tab[n, :, 1] = sn[:, sl]
        tab[n, :, 2] = cs[:, sl] / 16.0
        tab[n, :, 3] = sn[:, sl] / 16.0
    import ml_dtypes
    return c32, c16.astype(ml_dtypes.bfloat16), tab


NSLOT = 5
SLOT_B = 8192

IN_SPECS = [
    ("meta_tokens", [16, D]), ("ret_w_in", [1, D, 6144]), ("ret_w_out", [1, 2048, D]),
    ("dn_w_in", [1, D, 6160]), ("dn_w_out", [1, 2048, D]),
    ("ffn_w_gate", [2, D, FF]), ("ffn_w_up", [2, D, FF]), ("ffn_w_down", [2, FF, D]),
]
R_FIN, R_GNW, R_DNW, R_ALOG, R_DTB = 0, 1024, 1536, 1792, 1800
R_W = 1808
P_NCOL, P_CONV = 0, 32
P_W = 32 + 128


OFFS = {}


class Arena:
    def __init__(self):
        self.off = 0

    def alloc(self, nbytes, align=64):
        self.off = (self.off + align - 1) // align * align
        o = self.off
        self.off += nbytes
        return o


def build_program(NT, n_layers=2, dbg=None, stage=99):
    nc = bass.Bass("TRN2", target_bir_lowering=False)
    SEQ = (NT - 1) * T
    dram = {}
    x = nc.dram_tensor("x", [SEQ, D], F32, kind="ExternalInput").ap()
    for name, shp in IN_SPECS:
        dram[name] = nc.dram_tensor(name, shp, F32, kind="ExternalInput").ap()
    d_c32 = nc.dram_tensor("c32", [128, C_W32], F32, kind="ExternalInput").ap()
    d_c16 = nc.dram_tensor("c16", [128, C_W16], BF16, kind="ExternalInput").ap()
    d_tab = nc.dram_tensor("tab", [NT, 128, 4, T], F32, kind="ExternalInput").ap()
    d_rp = nc.dram_tensor("rpack", [1, R_W], F32, kind="ExternalInput").ap()
    d_pp = nc.dram_tensor("ppack", [128, P_W], F32, kind="ExternalInput").ap()
    y = nc.dram_tensor("y", [SEQ, D], F32, kind="ExternalOutput").ap()
    dbg_out = {}
    if dbg:
        for nm, shp in dbg.items():
            dbg_out[nm] = nc.dram_tensor("dbg_" + nm, shp, F32, kind="ExternalOutput").ap()
    s_rin = nc.dram_tensor("s_rin", [D, 6144], BF16, kind="Internal").ap()
    s_din = nc.dram_tensor("s_din", [D, 6160], BF16, kind="Internal").ap()
    s_rout = nc.dram_tensor("s_rout", [128, 8, 16, 128], BF16, kind="Internal").ap()
    s_dout = nc.dram_tensor("s_dout", [128, 8, 16, 128], BF16, kind="Internal").ap()
    s_g = [nc.dram_tensor("s_g%d" % l, [D, FF], BF16, kind="Internal").ap() for l in range(2)]
    s_u = [nc.dram_tensor("s_u%d" % l, [D, FF], BF16, kind="Internal").ap() for l in range(2)]
    s_dw = [nc.dram_tensor("s_dw%d" % l, [128, 8, NHC, 128], BF16, kind="Internal").ap() for l in range(2)]

    with ExitStack() as st:
        A = Arena()
        o_c32 = A.alloc(C_W32 * 4)
        o_c16 = A.alloc(C_W16 * 2)
        o_idb = A.alloc(256)
        o_oneb = A.alloc(256)
        o_rp = A.alloc(R_W * 4)
        o_pp = A.alloc(P_W * 4)
        o_nexpa = A.alloc(32)
        o_wba = A.alloc(8 * 16 * 2)
        o_hT = A.alloc(8 * T * 4)
        o_hnT = A.alloc(8 * T * 2)
        o_tab = A.alloc(4 * T * 4)
        o_ring = A.alloc(NSLOT * SLOT_B)
        o_A = A.alloc(16 * T * 2)
        o_B = A.alloc(NB * 2048 * 2)
        assert o_B == o_A + 16 * T * 2 and NB * 2048 * 2 >= (NHC - 16) * T * 2
        o_C = A.alloc(NB * 2048 * 2)
        o_X = A.alloc(NB * 1024 * 4)
        o_E = A.alloc(8 * T * 2)
        o_F = A.alloc(NB * 1024 * 2)
        o_Sr = A.alloc(8 * 512 * 4)
        o_Srb = A.alloc(8 * 512 * 2)
        o_Sd = A.alloc(8 * 256 * 4)
        o_Sdb = A.alloc(8 * 256 * 2)
        o_Sh = A.alloc(2 * 1024)
        o_sq = A.alloc(2 * T * 2)
        o_t0 = A.alloc(T * 4)
        o_t1 = A.alloc(T * 4)
        o_junk = A.alloc(1024 * 2)
        o_small = A.alloc(512)
        o_ft = A.alloc(2 * T * 4)
        o_cin = A.alloc(4 * (T + 4) * 4)
        o_hist = A.alloc(32 * 4 * 4)
        o_cacc = A.alloc(4 * T * 4)
        o_gts = A.alloc(NB * 8 * 4 * 4)
        o_ch = o_X
        assert o_F + NB * 1024 * 2 - o_X >= 8 * 2048
        o_p1 = A.alloc(NCH * 3 * 1024 + NB * 2 * 2048)
        o_bcx = A.alloc(6 * 1024)
        o_ktok = A.alloc(2048)
        o_vb = A.alloc(4096)
        o_vnew = A.alloc(4096)
        o_nwT = A.alloc(1024)
        o_eg = A.alloc(NCH * 32 + 64)
        o_osq = o_ch + 6 * 2048
        NBYTES = (A.off + 63) // 64 * 64
        OFFS.update({k: v for k, v in locals().items() if k.startswith('o_')})
        print('arena', NBYTES)
        assert NBYTES <= 207 * 1024, NBYTES

        sb_bf = st.enter_context(nc.sbuf_tensor("arena", [128, NBYTES // 2], BF16))
        sb = Mem({BF16: sb_bf, F32: sb_bf.bitcast(F32)}, NBYTES, 512)
        ps_f = st.enter_context(nc.psum_tensor("psum", [128, 4096], F32))
        pm = Mem({F32: ps_f, BF16: ps_f.bitcast(BF16)}, 16384, 2048)
        S = Sched(nc, st)

        ps_ptr = [0]

        def ps(nbytes, dt, shape, p0=0, p1=128):
            al = 2048 if nbytes <= 2048 else (8192 if nbytes > 4096 else 4096)
            o = (ps_ptr[0] + al - 1) // al * al
            if o + nbytes > 16384:
                o = 0
            ps_ptr[0] = o + nbytes
            return pm.view(o, dt, shape, p0, p1)

        c32 = sb.view(o_c32, F32, [C_W32])
        c16 = sb.view(o_c16, BF16, [C_W16])
        identf = c32[:, C_ID:C_ID + 128]
        onesf = c32[:, C_ONE:C_ONE + 128]
        identb = sb.view(o_idb, BF16, [128])
        onesb = sb.view(o_oneb, BF16, [128])
        rp = sb.view(o_rp, F32, [R_W])
        pp = sb.view(o_pp, F32, [P_W])
        nexpa = sb.view(o_nexpa, F32, [8])
        wba = sb.view(o_wba, BF16, [8, 16])
        hT = sb.view(o_hT, F32, [8, T])
        hnT = sb.view(o_hnT, BF16, [8, T])
        tab = sb.view(o_tab, F32, [4, T])
        xin = sb.view(o_X, F32, [NB, 1024])

        def cast_plain(dst, src):
            K, N = src.shape
            for r0 in range(0, K, 128):
                c0 = 0
                while c0 < N:
                    w_ = min(2048, N - c0)
                    cw_ = 512 if w_ % 512 == 0 else 16
                    S.dma("pool", "cast", [(dst[r0:r0 + 128, c0:c0 + w_].rearrange("p (a b) -> p a b", b=cw_),
                                            src[r0:r0 + 128, c0:c0 + w_].rearrange("p (a b) -> p a b", b=cw_))])
                    c0 += w_

        def cast_oc(dst, src, KC):
            sv = src.rearrange("(kc p) (oc c) -> oc p kc c", p=128, c=128)
            for oc in range(8):
                S.dma("pool", "cast", [(dst[:, oc, :, :], sv[oc])])

        cast_plain(s_rin, dram["ret_w_in"][0])
        cast_oc(s_rout, dram["ret_w_out"][0], 16)
        for l in range(2):
            cast_plain(s_g[l], dram["ffn_w_gate"][l])
            cast_plain(s_u[l], dram["ffn_w_up"][l])
            cast_oc(s_dw[l], dram["ffn_w_down"][l], NHC)
            if l == 0:
                cast_plain(s_din, dram["dn_w_in"][0])
                cast_oc(s_dout, dram["dn_w_out"][0], 16)

        if stage < 1:
            S.wait_dma_all("sp", "cast")
            return nc
        S.dma("sp", "const", [(c32, d_c32), (c16, d_c16), (pp, d_pp),
                              (rp, d_rp.broadcast_to([128, R_W]))], writes=[c32, c16, pp, rp])
        if stage < 1.1:
            S.wait_dma_all("act", "const")
            return nc
        S.copy("dve", identb, identf)
        S.copy("dve", onesb, onesf)
        if stage < 1.2:
            S._wait(S.E["act"], "dve", 2)
            return nc
        S.wait_dma_all("sp", "cast")
        S.dma("sp", "const2", [(wba, s_din.rearrange("(kc p) n -> p kc n", p=128)[:, :, 6144:6160])], writes=[wba])
        if stage < 1.5:
            S.wait_dma_all("act", "const2")
            return nc
        S.act(nexpa, rp[:, R_ALOG:R_ALOG + 8], AF.Exp)
        S.ts("dve", nexpa, nexpa, -1.0, ALU.mult)
        if stage < 2:
            S.wait_dma_all("act", "const")
            S.wait_dma_all("act", "const2")
            return nc
        Sr = sb.view(o_Sr, F32, [8, 512])
        Srb = sb.view(o_Srb, BF16, [8, 512])
        Sd = sb.view(o_Sd, F32, [8, 256])
        Sdb = sb.view(o_Sdb, BF16, [8, 256])
        for v_ in (Sr, Srb, Sd, Sdb):
            S.memset("pool", v_, 0.0)

        def wlist():
            L = []
            rin = s_rin.rearrange("(kc p) n -> p kc n", p=128)
            for g in range(12):
                L.append(("rin%d" % g, rin[:, :, g * 512:(g + 1) * 512], [8, 512]))
            for g in range(4):
                L.append(("rout%d" % g, s_rout[:, 2 * g:2 * g + 2, :, :], [2, 16, 128]))

            def ffn(l):
                gg = s_g[l].rearrange("(kc p) n -> p kc n", p=128)
                uu = s_u[l].rearrange("(kc p) n -> p kc n", p=128)
                for g in range(6):
                    ncol = min(512, FF - g * 512)
                    L.append(("g%d_%d" % (l, g), gg[:, :, g * 512:g * 512 + ncol], [8, ncol]))
                    L.append(("u%d_%d" % (l, g), uu[:, :, g * 512:g * 512 + ncol], [8, ncol]))
                for oc in range(8):
                    L.append(("dw%d_%d" % (l, oc), s_dw[l][:, oc, :, :], [NHC, 128]))
            ffn(0)
            if n_layers > 1:
                din = s_din.rearrange("(kc p) n -> p kc n", p=128)
                for g in range(12):
                    L.append(("din%d" % g, din[:, :, g * 512:(g + 1) * 512], [8, 512]))
                for g in range(4):
                    L.append(("dout%d" % g, s_dout[:, 2 * g:2 * g + 2, :, :], [2, 16, 128]))
                ffn(1)
            return L

        WL = wlist()
        NW = len(WL)
        wstate = {"issued": 0, "used": 0}
        total_w = NW * NT

        def w_issue():
            i = wstate["issued"]
            name, src, shp = WL[i % NW]
            slot = i % NSLOT
            dst = sb.view(o_ring + slot * SLOT_B, BF16, shp)
            S.dma("sp", "w%d" % slot, [(dst, src)], writes=[dst])
            wstate["issued"] += 1

        def wnext(name):
            i = wstate["used"]
            nm, src, shp = WL[i % NW]
            assert nm == name, (nm, name)
            while wstate["issued"] < min(total_w, i + NSLOT - 1):
                w_issue()
            wstate["used"] += 1
            return sb.view(o_ring + (i % NSLOT) * SLOT_B, BF16, shp)

        sqb = [sb.view(o_sq + k * T * 2, BF16, [T]) for k in range(2)]
        t0 = sb.view(o_t0, F32, [T])
        t1 = sb.view(o_t1, F32, [T])
        junk = sb.view(o_junk, BF16, [1024])
        small = sb.view(o_small, F32, [64])
        flip = [0]

        def evac(out, in_):
            flip[0] ^= 1
            S.copy("act" if flip[0] else "dve", out, in_)

        def rstd_from(ssq_v, out_v, mean_scale, p0=0, p1=128):
            S.act(out_v, ssq_v, AF.Ln, bias=epsc[p0:p1], scale=mean_scale)
            S.act(out_v, out_v, AF.Exp, scale=-0.5)

        epsc = sb.view(o_small + 256, F32, [1])
        S.memset("pool", epsc, EPS)
        onec = sb.view(o_small + 260, F32, [1])
        S.memset("pool", onec, 1.0)

        def rmsnorm(col0):
            ssq = ps(T * 4, F32, [T])
            for kc in range(8):
                S.act(sqb[kc % 2], hT[:, kc, :], AF.Square)
                S.mm(ssq, onesb, sqb[kc % 2], start=(kc == 0), stop=(kc == 7))
            rstd_from(ssq, t0, 1.0 / D)
            for kc in range(8):
                S.stt(hnT[:, kc, :], hT[:, kc, :], pp[:, P_NCOL + col0 + kc:P_NCOL + col0 + kc + 1], t0, ALU.mult, ALU.mult)

        def proj_fm(wg, ci, out_ps):
            for kc in range(8):
                S.mm(out_ps, wg[:, kc, ci * 128:(ci + 1) * 128], hnT[:, kc, :], start=(kc == 0), stop=(kc == 7))

        def proj_tm(wg, tb, out_ps, ncol=512):
            for kc in range(8):
                S.mm(out_ps, hnT[:, kc, tb * 128:(tb + 1) * 128], wg[:, kc, 0:ncol], start=(kc == 0), stop=(kc == 7))

        sgw = sb.view(o_C, BF16, [NB, 2048])
        ogT = sb.view(o_A, BF16, [16, T])

        def mixer_out(wname):
            for fc in range(16):
                tp = ps(T * 2, BF16, [T])
                for tb in range(NB):
                    S.tr(tp[:, tb * 128:(tb + 1) * 128], sgw[:, tb, fc * 128:(fc + 1) * 128], identb)
                evac(ogT[:, fc, :], tp)
            for g in range(4):
                wg = wnext("%s%d" % (wname, g))
                for o2 in range(2):
                    oc = 2 * g + o2
                    mp = ps(T * 4, F32, [T])
                    for fc in range(16):
                        S.mm(mp, wg[:, o2, fc, :], ogT[:, fc, :], start=(fc == 0), stop=(fc == 15))
                    S.tt("dve", hT[:, oc, :], mp, hT[:, oc, :], ALU.add)

        ffa = sb.view(o_A, BF16, [NHC, T])
        fts = [sb.view(o_ft + k * T * 4, F32, [T]) for k in range(2)]

        def ffn(l):
            rmsnorm(16 + l * 8)
            for g in range(6):
                ncol = min(512, FF - g * 512)
                wgg = wnext("g%d_%d" % (l, g))
                wgu = wnext("u%d_%d" % (l, g))
                for ci in range(ncol // 128):
                    hc = g * 4 + ci
                    gp = ps(T * 4, F32, [T])
                    proj_fm(wgg, ci, gp)
                    up = ps(T * 4, F32, [T])
                    proj_fm(wgu, ci, up)
                    ft = fts[hc % 2]
                    S.act(ft, gp, AF.Silu)
                    S.tt("dve", ffa[:, hc, :], up, ft, ALU.mult)
            for oc in range(8):
                wg = wnext("dw%d_%d" % (l, oc))
                mp = ps(T * 4, F32, [T])
                for hc in range(NHC):
                    S.mm(mp, wg[:, hc, :], ffa[:, hc, :], start=(hc == 0), stop=(hc == NHC - 1))
                S.tt("dve", hT[:, oc, :], mp, hT[:, oc, :], ALU.add)

        gam = _gammas()
        qkT = sb.view(o_A, BF16, [16, T])
        vtok = sb.view(o_B, BF16, [NB, 2048])
        qxiT = sb.view(o_E, BF16, [8, T])
        kz = sb.view(o_F, BF16, [NB, 1024])
        rtmp = [sb.view(o_X + k * T * 4, F32, [T]) for k in range(4)]
        Shb = [sb.view(o_Sh + k * 1024, BF16, [512]) for k in range(2)]
        gnw_row = rp[:, R_GNW:R_GNW + 512]

        def retention():
            rmsnorm(0)
            for g in range(4):
                wg = wnext("rin%d" % g)
                isk = g >= 2
                for hh in range(2):
                    h = (g % 2) * 2 + hh
                    p1 = ps(T * 4, F32, [T])
                    proj_fm(wg, hh * 2, p1)
                    p2 = ps(T * 4, F32, [T])
                    proj_fm(wg, hh * 2 + 1, p2)
                    cs = tab[:, 2 if isk else 0, :]
                    sn = tab[:, 3 if isk else 1, :]
                    dst = qkT[:, (8 if isk else 0) + h * 2, :]
                    dst2 = qkT[:, (8 if isk else 0) + h * 2 + 1, :]
                    S.tt("dve", rtmp[0], p1, cs, ALU.mult)
                    S.tt("dve", rtmp[1], p2, sn, ALU.mult)
                    S.tt("pool", dst, rtmp[0], rtmp[1], ALU.subtract)
                    S.tt("dve", rtmp[2], p1, sn, ALU.mult)
                    S.tt("dve", rtmp[3], p2, cs, ALU.mult)
                    S.tt("pool", dst2, rtmp[2], rtmp[3], ALU.add)
            for g in range(4):
                wg = wnext("rin%d" % (4 + g))
                for tb in range(NB):
                    vp = ps(2048, F32, [512])
                    proj_tm(wg, tb, vp)
                    evac(vtok[:, tb, g * 512:(g + 1) * 512], vp)
            for g in range(4):
                wg = wnext("rin%d" % (8 + g))
                for tb in range(NB):
                    gp = ps(2048, F32, [512])
                    proj_tm(wg, tb, gp)
                    dst = sgw[:, tb, g * 512:(g + 1) * 512]
                    S.act(dst, gp, AF.Silu)
                    S.tt("pool", dst, dst, gnw_row, ALU.mult)
            for c8 in range(8):
                h = c8 // 2
                S.tt("pool", qxiT[:, c8, :], qkT[:, c8, :], c16[:, C_XI + h * T:C_XI + (h + 1) * T], ALU.mult)
            for tb in range(NB):
                for h in range(4):
                    kp = ps(512, BF16, [256])
                    for half in range(2):
                        S.tr(kp[:, half * 128:(half + 1) * 128], qkT[:, 8 + h * 2 + half, tb * 128:(tb + 1) * 128], identb)
                    S.act(kz[:, tb, h * 256:(h + 1) * 256], kp, AF.Copy,
                          scale=c32[:, C_ZETA + tb * 4 + h:C_ZETA + tb * 4 + h + 1])
            for h in range(4):
                Sh = Shb[h % 2]
                offs = []
                o = 0
                for jb in range(NB):
                    N = T - jb * 128
                    sp_ = ps(N * 4, F32, [N])
                    for half in range(2):
                        S.mm(sp_, qkT[:, 8 + h * 2 + half, jb * 128:(jb + 1) * 128], qkT[:, h * 2 + half, jb * 128:T],
                             start=(half == 0), stop=(half == 1))
                    S.tt("dve", Sh[:, o:o + N], sp_, c16[:, C_D + h * NB * 128:C_D + h * NB * 128 + N], ALU.mult)
                    offs.append(o)
                    o += N
                for ib in range(NB):
                    op_ = ps(2048, F32, [512])
                    first = True
                    for jb in range(ib + 1):
                        S.mm(op_, Sh[:, offs[jb] + (ib - jb) * 128: offs[jb] + (ib - jb + 1) * 128],
                             vtok[:, jb, h * 512:(h + 1) * 512], start=first, stop=False)
                        first = False
                    for half in range(2):
                        S.mm(op_, qxiT[:, h * 2 + half, ib * 128:(ib + 1) * 128], Srb[:, h * 2 + half, :],
                             start=False, stop=(half == 1))
                    ssq = small[:, h * 4 + ib:h * 4 + ib + 1]
                    S.act(junk[:, 0:512], op_, AF.Square, accum_out=ssq)
                    rstd_from(ssq, ssq, 1.0 / 512)
                    dst = sgw[:, ib, h * 512:(h + 1) * 512]
                    S.stt(dst, op_, ssq, dst, ALU.mult, ALU.mult)
                for half in range(2):
                    dp = ps(2048, F32, [512])
                    for jb in range(NB):
                        S.mm(dp, kz[:, jb, h * 256 + half * 128:h * 256 + (half + 1) * 128],
                             vtok[:, jb, h * 512:(h + 1) * 512], start=(jb == 0), stop=(jb == NB - 1))
                    sv = Sr[:, h * 2 + half, :]
                    S.stt(sv, sv, float(gam[h] ** T), dp, ALU.mult, ALU.add)
                    S.copy("pool", Srb[:, h * 2 + half, :], sv)
            mixer_out("rout")


        qknT = sb.view(o_A, BF16, [16, T])
        vT = sb.view(o_B, BF16, [16, T])
        cins = [sb.view(o_cin + k * (T + 4) * 4, F32, [T + 4]) for k in range(4)]
        hist = sb.view(o_hist, F32, [32, 4])
        caccs = [sb.view(o_cacc + k * T * 4, F32, [T]) for k in range(4)]
        gts = sb.view(o_gts, F32, [4, NB, 8])
        chv = [sb.view(o_ch + k * 2048, F32, [8, 64]) for k in range(8)]
        NTp = [sb.view(o_p1 + c * 3072, BF16, [8, 64]) for c in range(NCH)]
        qkp = [sb.view(o_p1 + c * 3072 + 1024, BF16, [8, 64]) for c in range(NCH)]
        qgT = [sb.view(o_p1 + c * 3072 + 2048, BF16, [8, 64]) for c in range(NCH)]
        kbg = [sb.view(o_p1 + NCH * 3072 + tb * 4096, BF16, [8, 128]) for tb in range(NB)]
        kdec = [sb.view(o_p1 + NCH * 3072 + tb * 4096 + 2048, BF16, [8, 128]) for tb in range(NB)]
        bcx = [sb.view(o_bcx + k * 1024, BF16, [8, 64]) for k in range(6)]
        ktok = sb.view(o_ktok, BF16, [8, 128])
        vbf = [sb.view(o_vb, BF16, [8, 256]) for k in range(2)]
        vnew = sb.view(o_vnew, BF16, [8, 256])
        nwT = sb.view(o_nwT, BF16, [8, 64])
        egl = sb.view(o_eg, F32, [NCH, 8])
        sm8 = sb.view(o_eg + NCH * 32, F32, [16])
        osq = sb.view(o_osq, BF16, [2048])
        lnqc = sb.view(o_small + 264, F32, [1])
        S.memset("pool", lnqc, float(np.log(128.0 ** -0.5)))
        S.memset("pool", hist, 0.0)
        S.memset("pool", vnew, 0.0)
        for c in range(NCH):
            S.memset("pool", NTp[c], 0.0)
            S.memset("pool", qkp[c], 0.0)
        dnw_row = rp[:, R_DNW:R_DNW + 256]
        TLE = c32[:, C_TLE:C_TLE + 64]
        TGT = c32[:, C_TGT:C_TGT + 64]

        def deltanet(n):
            rmsnorm(8)
            for g in range(8):
                wg = wnext("din%d" % g)
                for ci in range(4):
                    cc = g * 4 + ci
                    pp_ = ps(T * 4, F32, [T])
                    proj_fm(wg, ci, pp_)
                    cin = cins[cc % 4]
                    acc = caccs[cc % 4]
                    S.copy("pool", cin[:, 0:3], hist[:, cc, 0:3])
                    S.act(cin[:, 3:3 + T], pp_, AF.Copy)
                    if n == 0:
                        S.memset("pool", cin[:, 3:3 + LPAD], 0.0)
                    S.copy("pool", hist[:, cc, 0:3], cin[:, T:T + 3])
                    cw = lambda t: pp[:, P_CONV + cc * 4 + t:P_CONV + cc * 4 + t + 1]
                    S.ts("dve", acc, cin[:, 3:3 + T], cw(3), ALU.mult)
                    for t in (2, 1, 0):
                        S.stt(acc, cin[:, t:t + T], cw(t), acc, ALU.mult, ALU.add)
                    if cc < 16:
                        S.act(qknT[:, cc, :], acc, AF.Silu)
                    else:
                        S.act(vT[:, cc - 16, :], acc, AF.Silu)
                if g == 1 or g == 3:
                    base = 0 if g == 1 else 8
                    blk = qknT[:, base:base + 8, :]
                    sqbig = sb.view(o_osq, BF16, [8, T])
                    S.act(sqbig, blk, AF.Square)
                    ssq8 = ps(8 * T * 4, F32, [8, T])
                    for h in range(8):
                        S.mm(ssq8[:, h, :], onesb, sqbig[:, h, :])
                    rst = sb.view(o_ch, F32, [8, T])
                    S.act(rst, ssq8, AF.Ln, bias=epsc, scale=1.0)
                    if base == 0:
                        S.act(rst, rst, AF.Exp, bias=lnqc, scale=-0.5)
                    else:
                        S.act(rst, rst, AF.Exp, scale=-0.5)
                    S.tt("dve", blk, blk, rst, ALU.mult)
            for g in range(4):
                wg = wnext("din%d" % (8 + g))
                for tb in range(NB):
                    gp = ps(2048, F32, [512])
                    proj_tm(wg, tb, gp)
                    dst = sgw[:, tb, g * 512:(g + 1) * 512]
                    S.act(dst, gp, AF.Silu)
                    d3 = dst.rr("p (h v) -> p h v", h=2)
                    S.tt("pool", d3, d3, dnw_row.us(1).bc([128, 2, 256]), ALU.mult)
            for tb in range(NB):
                bp = ps(64, F32, [16])
                for kc in range(8):
                    S.mm(bp, hnT[:, kc, tb * 128:(tb + 1) * 128], wba[:, kc, :], start=(kc == 0), stop=(kc == 7))
                S.act(gts[:, 1, tb, :], bp[:, 0:8], AF.Sigmoid)
                S.tt("dve", gts[:, 3, tb, :], bp[:, 8:16], rp[:, R_DTB:R_DTB + 8], ALU.add)
                S.act(gts[:, 3, tb, :], gts[:, 3, tb, :], AF.Exp)
                S.act(gts[:, 3, tb, :], gts[:, 3, tb, :], AF.Ln, bias=onec, scale=1.0)
                S.tt("dve", gts[:, 0, tb, :], gts[:, 3, tb, :], nexpa, ALU.mult)
                if n == 0:
                    vcol = c32[:, C_VAL + tb:C_VAL + tb + 1]
                    S.ts("dve", gts[:, 0, tb, :], gts[:, 0, tb, :], vcol, ALU.mult)
                    S.ts("dve", gts[:, 1, tb, :], gts[:, 1, tb, :], vcol, ALU.mult)
                S.ts("dve", gts[:, 2, tb, :], gts[:, 1, tb, :], -1.0, ALU.mult)
            for tb in range(NB):
                ktp = ps(2048, BF16, [8, 128])
                for h in range(8):
                    S.tr(ktp[:, h, :], qknT[:, 8 + h, tb * 128:(tb + 1) * 128], identb)
                S.copy("act", ktok, ktp)
                for half in range(2):
                    c = tb * 2 + half
                    P0, P1 = half * 64, half * 64 + 64
                    tok = slice(c * 64, (c + 1) * 64)
                    g_c = gts[P0:P1, 0, tb, :]
                    beta_c = gts[P0:P1, 1, tb, :]
                    nbeta_c = gts[P0:P1, 2, tb, :]
                    gtri1, decT, decS, tmul, egrow, gtri2 = chv[0][P0:P1], chv[1][P0:P1], chv[2][P0:P1], chv[3][P0:P1], chv[4], chv[5][P0:P1]
                    S.tt("pool", gtri1, g_c.us(2).bc([64, 8, 64]), TLE[P0:P1].us(1).bc([64, 8, 64]), ALU.mult)
                    S.tt("pool", gtri2, g_c.us(2).bc([64, 8, 64]), TGT[P0:P1].us(1).bc([64, 8, 64]), ALU.mult)
                    g1f = gtri1.rr("p h i -> p (h i)")
                    g2f = gtri2.rr("p h i -> p (h i)")
                    id64 = identf[P0:P1, P0:P1]
                    gr_ps = ps(2048, F32, [512])
                    S.mm(gr_ps, onesf[P0:P1, :], g1f)
                    dT_ps = ps(2048, F32, [512], P0, P1)
                    S.mm(dT_ps, TGT[P0:P1], g1f, start=True, stop=False)
                    S.mm(dT_ps, id64, c32[P0:P1, C_MT:C_MT + 512], start=False, stop=True)
                    dS_ps = ps(2048, F32, [512], P0, P1)
                    S.mm(dS_ps, TLE[P0:P1], g2f, start=True, stop=False)
                    S.mm(dS_ps, id64, c32[P0:P1, C_MS:C_MS + 512], start=False, stop=True)
                    gc_ps = ps(32, F32, [8], P0, P1)
                    S.mm(gc_ps, TLE[P0:P1], g_c)
                    S.act(decT.rr("p h i -> p (h i)"), dT_ps, AF.Exp)
                    S.act(decS.rr("p h i -> p (h i)"), dS_ps, AF.Exp)
                    S.act(egrow.rr("p h i -> p (h i)"), gr_ps, AF.Exp)
                    gcol = sm8[P0:P1, 0:8]
                    dcol = sm8[P0:P1, 8:16]
                    S.copy("act", gcol, gc_ps)
                    S.tt("dve", dcol, gr_ps[P0:P1].rr("p (h i) -> p h i", h=8)[:, :, 63], gcol, ALU.subtract)
                    S.act(dcol, dcol, AF.Exp)
                    S.act(gcol, gcol, AF.Exp)
                    S.tt("dve", gcol, gcol, beta_c, ALU.mult)
                    S.copy("pool", egl[:, c, :], egrow[:, :, 63])
                    kk_ps = ps(2048, F32, [8, 64], P0, P1)
                    qk_ps = ps(2048, F32, [8, 64], P0, P1)
                    for h in range(8):
                        S.mm(kk_ps[:, h, :], qknT[:, 8 + h, tok], qknT[:, 8 + h, tok])
                    for h in range(8):
                        S.mm(qk_ps[:, h, :], qknT[:, 8 + h, tok], qknT[:, h, tok])
                    S.tt("pool", tmul, decS, nbeta_c.us(2).bc([64, 8, 64]), ALU.mult)
                    B, C, X = bcx[0][P0:P1], bcx[1][P0:P1], bcx[2][P0:P1]
                    S.tt("dve", B, kk_ps, tmul, ALU.mult)
                    S.tt("dve", qkp[c][P0:P1], qk_ps, decT, ALU.mult)
                    ct_ps = ps(1024, BF16, [8, 64], P0, P1)
                    idb64 = identb[P0:P1, P0:P1]
                    for h in range(8):
                        S.tr(ct_ps[:, h, :], B[:, h, :], idb64)
                    S.copy("act", C, ct_ps)
                    S.tt("dve", X, ct_ps, id64.us(1).bc([64, 8, 64]), ALU.add)
                    for k in range(1, 6):
                        Bn, Cn, Xn = bcx[3 * (k % 2)][P0:P1], bcx[3 * (k % 2) + 1][P0:P1], bcx[3 * (k % 2) + 2][P0:P1]
                        if k == 5:
                            Xn = NTp[c][P0:P1]
                        b_ps = ps(2048, F32, [8, 64], P0, P1)
                        for h in range(8):
                            S.mm(b_ps[:, h, :], C[:, h, :], B[:, h, :])
                        if k < 5:
                            c_ps = ps(2048, F32, [8, 64], P0, P1)
                            for h in range(8):
                                S.mm(c_ps[:, h, :], B[:, h, :], C[:, h, :])
                        S.copy("act", Bn, b_ps)
                        if k < 5:
                            S.copy("dve", Cn, c_ps)
                        x_ps = ps(2048, F32, [8, 64], P0, P1)
                        for h in range(8):
                            S.mm(x_ps[:, h, :], Bn[:, h, :], X[:, h, :])
                        S.tt("dve", Xn, x_ps, X, ALU.add)
                        B, C, X = Bn, Cn, Xn
                    S.tt("pool", kbg[tb][P0:P1], ktok[P0:P1], gcol.us(2).bc([64, 8, 128]), ALU.mult)
                    S.tt("pool", kdec[tb][P0:P1], ktok[P0:P1], dcol.us(2).bc([64, 8, 128]), ALU.mult)
                    S.tt("pool", qgT[c], qknT[:, 0:8, tok], egrow, ALU.mult)
            for tb in range(NB):
                vtp = ps(4096, BF16, [16, 128])
                for vc in range(16):
                    S.tr(vtp[:, vc, :], vT[:, vc, tb * 128:(tb + 1) * 128], identb)
                vb = vbf[tb % 2]
                S.tt("dve", vb, vtp.rr("p (h a) c -> p h (a c)", h=8), gts[:, 1, tb, :].us(2).bc([128, 8, 256]), ALU.mult)
                for half in range(2):
                    c = tb * 2 + half
                    P0, P1 = half * 64, half * 64 + 64
                    w_ps = ps(2048, F32, [8, 64])
                    for h in range(8):
                        S.mm(w_ps[:, h, :], kbg[tb][P0:P1, h, :], NTp[c][P0:P1, h, :])
                    S.act(nwT, w_ps, AF.Copy, scale=-1.0)
                    vn_ps = ps(8192, F32, [8, 256], P0, P1)
                    for h in range(8):
                        S.mm(vn_ps[:, h, :], NTp[c][:, h, :], vb[:, h, :], start=True, stop=False)
                        S.mm(vn_ps[:, h, :], nwT[:, h, :], Sdb[:, h, :], start=False, stop=True)
                    S.copy("act", vnew[P0:P1, 0:4, :], vn_ps[:, 0:4, :])
                    S.copy("dve", vnew[P0:P1, 4:8, :], vn_ps[:, 4:8, :])
                    d_ps = ps(8192, F32, [8, 256])
                    for h in range(8):
                        S.mm(d_ps[:, h, :], kdec[tb][P0:P1, h, :], vnew[P0:P1, h, :])
                    o_ps = ps(8192, F32, [8, 256], P0, P1)
                    for h in range(8):
                        S.mm(o_ps[:, h, :], qkp[c][:, h, :], vnew[:, h, :], start=True, stop=False)
                        S.mm(o_ps[:, h, :], qgT[c][:, h, :], Sdb[:, h, :], start=False, stop=True)
                    for h in range(8):
                        S.stt(Sd[:, h, :], Sd[:, h, :], egl[:, c, h:h + 1], d_ps[:, h, :], ALU.mult, ALU.add)
                        S.copy("pool" if h % 2 else "act", Sdb[:, h, :], Sd[:, h, :])
                    S.act(osq[P0:P1], o_ps.rr("p h v -> p (h v)"), AF.Square)
                    ss8 = sm8[P0:P1, 0:8]
                    S.reduce_add(ss8, osq[P0:P1].rr("p (h v) -> p h v", h=8))
                    rstd_from(ss8, ss8, 1.0 / 256, P0, P1)
                    for h in range(8):
                        dst = sgw[P0:P1, tb, h * 256:(h + 1) * 256]
                        S.stt(dst, o_ps[:, h, :], ss8[:, h:h + 1], dst, ALU.mult, ALU.mult)
            mixer_out("dout")

        finw_row = rp[:, R_FIN:R_FIN + 1024]
        for n in range(NT):
            if n == 0:
                S.memset("pool", xin, 0.0)
                S.dma("sp", "xin", [(xin[112:128, NB - 1, :], dram["meta_tokens"])], writes=[xin])
            else:
                S.dma("sp", "xin", [(xin, x[(n - 1) * T:n * T, :].rearrange("(tb p) d -> p tb d", p=128))], writes=[xin])
            S.dma("sp", "tab", [(tab, d_tab[n])], writes=[tab])
            for kc in range(8):
                tp = ps(T * 4, F32, [T])
                for tb in range(NB):
                    S.tr(tp[:, tb * 128:(tb + 1) * 128], xin[:, tb, kc * 128:(kc + 1) * 128], identf)
                evac(hT[:, kc, :], tp)
            if stage < 3:
                S.dma("sp", "yout", [(y[0:128, :], hT.rr("p a b -> p (a b)")[:, 0:1024])], reads=[hT])
                break
            retention()
            if stage < 4:
                S.dma("sp", "yout", [(y[0:128, :], hT.rr("p a b -> p (a b)")[:, 0:1024])], reads=[hT])
                break
            ffn(0)
            if n_layers > 1:
                deltanet(n)
                ffn(1)
            if n >= 1:
                for tb in range(NB):
                    fp = ps(4096, F32, [1024])
                    for kc in range(8):
                        S.tr(fp[:, kc * 128:(kc + 1) * 128], hT[:, kc, tb * 128:(tb + 1) * 128], identf)
                    ssq = small[:, 32 + tb:33 + tb]
                    S.act(junk, fp, AF.Square, accum_out=ssq)
                    rstd_from(ssq, ssq, 1.0 / D)
                    S.stt(xin[:, tb, :], fp, ssq, finw_row, ALU.mult, ALU.mult)
                S.dma("sp", "yout", [(y[(n - 1) * T:n * T, :].rearrange("(tb p) d -> p tb d", p=128), xin)], reads=[xin])
        assert stage < 99 or wstate["used"] == total_w
        if "yout" in S.dsem:
            S.wait_dma_all("sp", "yout")
        for nm, ap_ in dbg_out.items():
            pass
        print("instructions:", S.n_ins, "arena bytes:", NBYTES)
    return nc


def make_in_map(inp, x_core, NT):
    c32, c16, tab = host_consts(NT)
    f = lambda a: np.ascontiguousarray(np.asarray(a, dtype=np.float32))
    m = {"x": f(x_core)}
    for name, shp in IN_SPECS:
        m[name] = f(inp[name]).reshape(shp)
    m["c32"] = c32
    m["c16"] = c16
    m["tab"] = tab
    rp = np.zeros((1, R_W), np.float32)
    rp[0, R_FIN:R_FIN + 1024] = f(inp["final_norm_w"])
    rp[0, R_GNW:R_GNW + 512] = f(inp["ret_gn_w"]).reshape(-1)
    rp[0, R_DNW:R_DNW + 256] = f(inp["dn_norm_w"]).reshape(-1)
    rp[0, R_ALOG:R_ALOG + 8] = f(inp["dn_a_log"]).reshape(-1)
    rp[0, R_DTB:R_DTB + 8] = f(inp["dn_dt_bias"]).reshape(-1)
    m["rpack"] = rp
    pp = np.zeros((128, P_W), np.float32)
    nv = [f(inp["mix_norm_w"])[0], f(inp["mix_norm_w"])[1], f(inp["ffn_norm_w"])[0], f(inp["ffn_norm_w"])[1]]
    for vi, v in enumerate(nv):
        pp[:, P_NCOL + vi * 8:P_NCOL + vi * 8 + 8] = v.reshape(8, 128).T
    cw = f(inp["dn_conv_w"])[0]
    pp[:, P_CONV:P_CONV + 128] = cw.reshape(4, 32, 128).transpose(2, 1, 0).reshape(128, 128)
    m["ppack"] = pp
    return m


_NC_CACHE = {}


def kernel(x, meta_tokens, mix_norm_w, ffn_norm_w, ret_w_in, ret_gn_w, ret_w_out,
           dn_w_in, dn_conv_w, dn_a_log, dn_dt_bias, dn_norm_w, dn_w_out,
           ffn_w_gate, ffn_w_up, ffn_w_down, final_norm_w):
    x = np.asarray(x)
    B, SEQ, _ = x.shape
    assert SEQ % T == 0
    NT = SEQ // T + 1
    inp = dict(meta_tokens=meta_tokens, mix_norm_w=mix_norm_w, ffn_norm_w=ffn_norm_w, ret_w_in=ret_w_in,
               ret_gn_w=ret_gn_w, ret_w_out=ret_w_out, dn_w_in=dn_w_in, dn_conv_w=dn_conv_w,
               dn_a_log=dn_a_log, dn_dt_bias=dn_dt_bias, dn_norm_w=dn_norm_w, dn_w_out=dn_w_out,
               ffn_w_gate=ffn_w_gate, ffn_w_up=ffn_w_up, ffn_w_down=ffn_w_down, final_norm_w=final_norm_w)
    inp = {k: np.asarray(v) for k, v in inp.items()}
    if NT not in _NC_CACHE:
        _NC_CACHE[NT] = build_program(NT, n_layers=2)
    nc = _NC_CACHE[NT]
    base = make_in_map(inp, x[0], NT)
    in_maps = []
    for b in range(B):
        m = dict(base)
        m["x"] = np.ascontiguousarray(x[b], dtype=np.float32)
        in_maps.append(m)
    res = run_bass_kernel_spmd(nc, in_maps, core_ids=list(range(B)))
    return np.stack([np.asarray(r["y"]) for r in res.results], axis=0).astype(np.float32)
```

```python
import numpy as np
from contextlib import ExitStack
import concourse.bass as bass
import concourse.mybir as mybir
from concourse.bass_utils import run_bass_kernel_spmd

F32 = mybir.dt.float32
BF16 = mybir.dt.bfloat16
AF = mybir.ActivationFunctionType
ALU = mybir.AluOpType

T = 256
NB = T // 128
NCH = T // 64
LPAD = 240
D = 1024
FF = 2816
NHC = FF // 128
EPS = 1e-6
NEG = -30000.0
SELF_SYNC = True


def _esz(dt):
    return 2 if dt == BF16 else 4


class Region:
    __slots__ = ("w", "r")

    def __init__(self):
        self.w = None
        self.r = {}


class Mem:
    def __init__(self, handles, nbytes, gran):
        self.h = handles
        self.gran = gran
        ng = (nbytes + gran - 1) // gran
        self.regs = [[Region() for _ in range(ng)] for _ in range(2)]

    def view(self, off, dt, shape, p0=0, p1=128):
        e = _esz(dt)
        assert off % e == 0
        n = int(np.prod(shape))
        ap = self.h[dt][p0:p1, off // e: off // e + n]
        if len(shape) == 2:
            ap = ap.rearrange("p (a b) -> p a b", a=shape[0])
        elif len(shape) == 3:
            ap = ap.rearrange("p (a b c) -> p a b c", a=shape[0], b=shape[1])
        return V(ap, self)


class V:
    __slots__ = ("ap", "mem")

    def __init__(self, ap, mem):
        self.ap = ap
        self.mem = mem

    def __getitem__(self, idx):
        return V(self.ap[idx], self.mem)

    def rr(self, pat, **kw):
        return V(self.ap.rearrange(pat, **kw), self.mem)

    def bc(self, shape):
        return V(self.ap.broadcast_to(list(shape)), self.mem)

    def us(self, axis):
        return V(self.ap.unsqueeze(axis), self.mem)

    def regions(self):
        ap = self.ap
        e = _esz(ap.dtype)
        pat = ap.ap
        rowlen = pat[0][0]
        col = ap.offset % rowlen if rowlen > 0 else ap.offset
        ext = 1
        for (s, c) in pat[1:]:
            ext += (c - 1) * abs(s)
        b0 = col * e
        b1 = (col + ext) * e
        g = self.mem.gran
        pstart = ap.offset // rowlen if rowlen > 0 else 0
        pend = pstart + pat[0][1]
        out = []
        if pstart < 64:
            out += self.mem.regs[0][b0 // g: (b1 - 1) // g + 1]
        if pend > 64:
            out += self.mem.regs[1][b0 // g: (b1 - 1) // g + 1]
        return out


def _ap(x):
    return x.ap if isinstance(x, V) else x


class EngState:
    def __init__(self, name, eng, sem):
        self.name, self.eng, self.sem = name, eng, sem
        self.count = 0
        self.seen = {}


class Sched:
    def __init__(self, nc, stack):
        self.nc = nc
        self.stack = stack
        self.E = {}
        for name, eng in (("pe", nc.tensor), ("act", nc.scalar), ("dve", nc.vector),
                          ("pool", nc.gpsimd), ("sp", nc.sync)):
            sem = stack.enter_context(nc.semaphore("s_" + name))
            self.E[name] = EngState(name, eng, sem)
        self.dsem = {}
        self.n_ins = 0

    def sem_of(self, key):
        if isinstance(key, tuple):
            return self.dsem[key[1]][0]
        return self.E[key].sem

    def _wait(self, E, key, val):
        if E.seen.get(key, 0) >= val:
            return
        if key == E.name:
            if E.name in ("pe", "sp") or not SELF_SYNC:
                return
        E.eng.wait_ge(self.sem_of(key), val)
        E.seen[key] = val

    def _deps(self, E, reads, writes):
        deps = {}
        for v in reads:
            for rg in v.regions():
                if rg.w is not None:
                    k, val = rg.w
                    if deps.get(k, 0) < val:
                        deps[k] = val
        for v in writes:
            for rg in v.regions():
                if rg.w is not None:
                    k, val = rg.w
                    if deps.get(k, 0) < val:
                        deps[k] = val
                for k, val in rg.r.items():
                    if deps.get(k, 0) < val:
                        deps[k] = val
        for k, val in deps.items():
            self._wait(E, k, val)

    def op(self, en, fn, reads=(), writes=()):
        E = self.E[en]
        reads = [v for v in reads if isinstance(v, V)]
        writes = [v for v in writes if isinstance(v, V)]
        self._deps(E, reads, writes)
        ins = fn(E.eng)
        E.count += 1
        ins.then_inc(E.sem, 1)
        self.n_ins += 1
        key = E.name
        for v in reads:
            for rg in v.regions():
                rg.r[key] = E.count
        for v in writes:
            for rg in v.regions():
                rg.w = (key, E.count)
                rg.r = {}
        return ins

    def dma(self, qn, semname, pairs, reads=(), writes=(), **kw):
        Q = self.E[qn]
        if semname not in self.dsem:
            self.dsem[semname] = [self.stack.enter_context(self.nc.semaphore("d_" + semname)), 0]
        ds = self.dsem[semname]
        reads = [v for v in reads if isinstance(v, V)]
        writes = [v for v in writes if isinstance(v, V)]
        self._deps(Q, reads, writes)
        for (o, i) in pairs:
            Q.eng.dma_start(out=_ap(o), in_=_ap(i), **kw).then_inc(ds[0], 16)
            ds[1] += 16
        key = ("d", semname)
        for v in reads:
            for rg in v.regions():
                rg.r[key] = ds[1]
        for v in writes:
            for rg in v.regions():
                rg.w = (key, ds[1])
                rg.r = {}

    def wait_dma_all(self, en, semname):
        E = self.E[en]
        ds = self.dsem[semname]
        self._wait(E, ("d", semname), ds[1])

    def mm(self, out, lhsT, rhs, start=True, stop=True):
        return self.op("pe", lambda e: e.matmul(_ap(out), lhsT=_ap(lhsT), rhs=_ap(rhs), start=start, stop=stop),
                       reads=[lhsT, rhs], writes=[out])

    def tr(self, out, in_, ident):
        return self.op("pe", lambda e: e.transpose(out=_ap(out), in_=_ap(in_), identity=_ap(ident)),
                       reads=[in_, ident], writes=[out])

    def act(self, out, in_, func, bias=None, scale=None, accum_out=None, en="act"):
        kw = {}
        rd = [in_]
        if bias is not None:
            kw["bias"] = _ap(bias) if isinstance(bias, V) else bias
            rd.append(bias)
        if scale is not None:
            kw["scale"] = _ap(scale) if isinstance(scale, V) else scale
            rd.append(scale)
        wr = [out]
        if accum_out is not None:
            kw["accum_out"] = _ap(accum_out)
            wr.append(accum_out)
        return self.op("act", lambda e: e.activation(out=_ap(out), in_=_ap(in_), func=func, **kw),
                       reads=rd, writes=wr)

    def tt(self, en, out, in0, in1, op):
        return self.op(en, lambda e: e.tensor_tensor(out=_ap(out), in0=_ap(in0), in1=_ap(in1), op=op),
                       reads=[in0, in1], writes=[out])

    def ts(self, en, out, in0, s1, op0, s2=None, op1=None):
        rd = [in0, s1, s2]
        a1 = _ap(s1) if isinstance(s1, V) else s1
        a2 = _ap(s2) if isinstance(s2, V) else s2
        if op1 is None:
            return self.op(en, lambda e: e.tensor_scalar(out=_ap(out), in0=_ap(in0), scalar1=a1, scalar2=None, op0=op0),
                           reads=rd, writes=[out])
        return self.op(en, lambda e: e.tensor_scalar(out=_ap(out), in0=_ap(in0), scalar1=a1, scalar2=a2, op0=op0, op1=op1),
                       reads=rd, writes=[out])

    def stt(self, out, in0, scalar, in1, op0, op1):
        sc = _ap(scalar) if isinstance(scalar, V) else scalar
        return self.op("dve", lambda e: e.scalar_tensor_tensor(out=_ap(out), in0=_ap(in0), scalar=sc, in1=_ap(in1),
                                                                op0=op0, op1=op1),
                       reads=[in0, scalar, in1], writes=[out])

    def copy(self, en, out, in_):
        if en == "act":
            return self.act(out, in_, AF.Copy)
        return self.op(en, lambda e: e.tensor_copy(out=_ap(out), in_=_ap(in_)), reads=[in_], writes=[out])

    def memset(self, en, out, val):
        return self.op(en, lambda e: e.memset(_ap(out), val), writes=[out])

    def reduce_add(self, out, in_):
        return self.op("dve", lambda e: e.tensor_reduce(out=_ap(out), in_=_ap(in_), axis=mybir.AxisListType.X, op=ALU.add),
                       reads=[in_], writes=[out])


RET_H, RET_DK, RET_DV = 4, 256, 512
DN_H, DN_DK, DN_DV = 8, 128, 256


def _gammas():
    return (1.0 - np.exp2(-5.0 - np.arange(RET_H, dtype=np.float64)))


C_ID, C_ONE, C_TLE, C_TGT = 0, 128, 256, 320
C_MT, C_MS = 384, 896
C_ZETA = 1408
C_VAL = C_ZETA + NB * 4
C_W32 = C_VAL + NB
C_D, C_XI = 0, 4 * NB * 128
C_W16 = C_XI + 4 * T


def host_consts(NT):
    g = _gammas()
    c32 = np.zeros((128, C_W32), np.float32)
    c32[:, C_ID:C_ID + 128] = np.eye(128)
    c32[:, C_ONE:C_ONE + 128] = 1.0
    k = np.arange(128) % 64
    i = np.arange(64)
    c32[:, C_TLE:C_TLE + 64] = (k[:, None] <= i[None, :])
    c32[:, C_TGT:C_TGT + 64] = (i[None, :] < k[:, None])
    mt = np.where(i[None, :] < k[:, None], NEG, 0.0)
    ms = np.where(i[None, :] >= k[:, None], NEG, 0.0)
    c32[:, C_MT:C_MT + 512] = np.tile(mt, (1, 8))
    c32[:, C_MS:C_MS + 512] = np.tile(ms, (1, 8))
    p = np.arange(128)
    for tb in range(NB):
        for h in range(4):
            c32[:, C_ZETA + tb * 4 + h] = g[h] ** (T - 1 - (tb * 128 + p))
        c32[:, C_VAL + tb] = ((tb * 128 + p) >= LPAD)
    c16 = np.zeros((128, C_W16), np.float32)
    ii = np.arange(NB * 128)
    for h in range(4):
        dl = ii[None, :] - p[:, None]
        c16[:, C_D + h * NB * 128: C_D + (h + 1) * NB * 128] = np.where(dl >= 0, g[h] ** np.maximum(dl, 0), 0.0)
        c16[:, C_XI + h * T: C_XI + (h + 1) * T] = g[h] ** (np.arange(T) + 1.0)
    half = 128
    inv_freq = (10000.0 ** (-np.arange(half, dtype=np.float32) / half)).astype(np.float32)
    pos = (np.arange(NT * T) - LPAD).astype(np.float32)
    ang = (pos[:, None] * inv_freq[None, :]).astype(np.float32)
    cs = np.cos(ang.astype(np.float64)).T
    sn = np.sin(ang.astype(np.float64)).T
    tab = np.zeros((NT, 128, 4, T), np.float32)
    for n in range(NT):
        sl = slice(n * T, (n + 1) * T)
        tab[n, :, 0] = cs[:, sl]
        tab[n, :, 1] = sn[:, sl]
        tab[n, :, 2] = cs[:, sl] / 16.0
        tab[n, :, 3] = sn[:, sl] / 16.0
    import ml_dtypes
    return c32, c16.astype(ml_dtypes.bfloat16), tab


NSLOT = 5
SLOT_B = 8192

IN_SPECS = [
    ("meta_tokens", [16, D]), ("ret_w_in", [1, D, 6144]), ("ret_w_out", [1, 2048, D]),
    ("dn_w_in", [1, D, 6160]), ("dn_w_out", [1, 2048, D]),
    ("ffn_w_gate", [2, D, FF]), ("ffn_w_up", [2, D, FF]), ("ffn_w_down", [2, FF, D]),
]
R_FIN, R_GNW, R_DNW, R_ALOG, R_DTB = 0, 1024, 1536, 1792, 1800
R_W = 1808
P_NCOL, P_CONV = 0, 32
P_W = 32 + 128


OFFS = {}


class Arena:
    def __init__(self):
        self.off = 0

    def alloc(self, nbytes, align=64):
        self.off = (self.off + align - 1) // align * align
        o = self.off
        self.off += nbytes
        return o


def build_program(NT, n_layers=2, dbg=None, stage=99):
    nc = bass.Bass("TRN2", target_bir_lowering=False)
    SEQ = (NT - 1) * T
    dram = {}
    x = nc.dram_tensor("x", [SEQ, D], F32, kind="ExternalInput").ap()
    for name, shp in IN_SPECS:
        dram[name] = nc.dram_tensor(name, shp, F32, kind="ExternalInput").ap()
    d_c32 = nc.dram_tensor("c32", [128, C_W32], F32, kind="ExternalInput").ap()
    d_c16 = nc.dram_tensor("c16", [128, C_W16], BF16, kind="ExternalInput").ap()
    d_tab = nc.dram_tensor("tab", [NT, 128, 4, T], F32, kind="ExternalInput").ap()
    d_rp = nc.dram_tensor("rpack", [1, R_W], F32, kind="ExternalInput").ap()
    d_pp = nc.dram_tensor("ppack", [128, P_W], F32, kind="ExternalInput").ap()
    y = nc.dram_tensor("y", [SEQ, D], F32, kind="ExternalOutput").ap()
    dbg_out = {}
    if dbg:
        for nm, shp in dbg.items():
            dbg_out[nm] = nc.dram_tensor("dbg_" + nm, shp, F32, kind="ExternalOutput").ap()
    s_rin = nc.dram_tensor("s_rin", [D, 6144], BF16, kind="Internal").ap()
    s_din = nc.dram_tensor("s_din", [D, 6160], BF16, kind="Internal").ap()
    s_rout = nc.dram_tensor("s_rout", [128, 8, 16, 128], BF16, kind="Internal").ap()
    s_dout = nc.dram_tensor("s_dout", [128, 8, 16, 128], BF16, kind="Internal").ap()
    s_g = [nc.dram_tensor("s_g%d" % l, [D, FF], BF16, kind="Internal").ap() for l in range(2)]
    s_u = [nc.dram_tensor("s_u%d" % l, [D, FF], BF16, kind="Internal").ap() for l in range(2)]
    s_dw = [nc.dram_tensor("s_dw%d" % l, [128, 8, NHC, 128], BF16, kind="Internal").ap() for l in range(2)]

    with ExitStack() as st:
        A = Arena()
        o_c32 = A.alloc(C_W32 * 4)
        o_c16 = A.alloc(C_W16 * 2)
        o_idb = A.alloc(256)
        o_oneb = A.alloc(256)
        o_rp = A.alloc(R_W * 4)
        o_pp = A.alloc(P_W * 4)
        o_nexpa = A.alloc(32)
        o_wba = A.alloc(8 * 16 * 2)
        o_hT = A.alloc(8 * T * 4)
        o_hnT = A.alloc(8 * T * 2)
        o_tab = A.alloc(4 * T * 4)
        o_ring = A.alloc(NSLOT * SLOT_B)
        o_A = A.alloc(16 * T * 2)
        o_B = A.alloc(NB * 2048 * 2)
        assert o_B == o_A + 16 * T * 2 and NB * 2048 * 2 >= (NHC - 16) * T * 2
        o_C = A.alloc(NB * 2048 * 2)
        o_X = A.alloc(NB * 1024 * 4)
        o_E = A.alloc(8 * T * 2)
        o_F = A.alloc(NB * 1024 * 2)
        o_Sr = A.alloc(8 * 512 * 4)
        o_Srb = A.alloc(8 * 512 * 2)
        o_Sd = A.alloc(8 * 256 * 4)
        o_Sdb = A.alloc(8 * 256 * 2)
        o_Sh = A.alloc(2 * 1024)
        o_sq = A.alloc(2 * T * 2)
        o_t0 = A.alloc(T * 4)
        o_t1 = A.alloc(T * 4)
        o_junk = A.alloc(1024 * 2)
        o_small = A.alloc(512)
        o_ft = A.alloc(2 * T * 4)
        o_cin = A.alloc(4 * (T + 4) * 4)
        o_hist = A.alloc(32 * 4 * 4)
        o_cacc = A.alloc(4 * T * 4)
        o_gts = A.alloc(NB * 8 * 4 * 4)
        o_ch = o_X
        assert o_F + NB * 1024 * 2 - o_X >= 8 * 2048
        o_p1 = A.alloc(NCH * 3 * 1024 + NB * 2 * 2048)
        o_bcx = A.alloc(6 * 1024)
        o_ktok = A.alloc(2048)
        o_vb = A.alloc(4096)
        o_vnew = A.alloc(4096)
        o_nwT = A.alloc(1024)
        o_eg = A.alloc(NCH * 32 + 64)
        o_osq = o_ch + 6 * 2048
        NBYTES = (A.off + 63) // 64 * 64
        OFFS.update({k: v for k, v in locals().items() if k.startswith('o_')})
        print('arena', NBYTES)
        assert NBYTES <= 207 * 1024, NBYTES

        sb_bf = st.enter_context(nc.sbuf_tensor("arena", [128, NBYTES // 2], BF16))
        sb = Mem({BF16: sb_bf, F32: sb_bf.bitcast(F32)}, NBYTES, 512)
        ps_f = st.enter_context(nc.psum_tensor("psum", [128, 4096], F32))
        pm = Mem({F32: ps_f, BF16: ps_f.bitcast(BF16)}, 16384, 2048)
        S = Sched(nc, st)

        ps_ptr = [0]

        def ps(nbytes, dt, shape, p0=0, p1=128):
            al = 2048 if nbytes <= 2048 else (8192 if nbytes > 4096 else 4096)
            o = (ps_ptr[0] + al - 1) // al * al
            if o + nbytes > 16384:
                o = 0
            ps_ptr[0] = o + nbytes
            return pm.view(o, dt, shape, p0, p1)

        c32 = sb.view(o_c32, F32, [C_W32])
        c16 = sb.view(o_c16, BF16, [C_W16])
        identf = c32[:, C_ID:C_ID + 128]
        onesf = c32[:, C_ONE:C_ONE + 128]
        identb = sb.view(o_idb, BF16, [128])
        onesb = sb.view(o_oneb, BF16, [128])
        rp = sb.view(o_rp, F32, [R_W])
        pp = sb.view(o_pp, F32, [P_W])
        nexpa = sb.view(o_nexpa, F32, [8])
        wba = sb.view(o_wba, BF16, [8, 16])
        hT = sb.view(o_hT, F32, [8, T])
        hnT = sb.view(o_hnT, BF16, [8, T])
        tab = sb.view(o_tab, F32, [4, T])
        xin = sb.view(o_X, F32, [NB, 1024])

        def cast_plain(dst, src):
            K, N = src.shape
            for r0 in range(0, K, 128):
                c0 = 0
                while c0 < N:
                    w_ = min(2048, N - c0)
                    cw_ = 512 if w_ % 512 == 0 else 16
                    S.dma("pool", "cast", [(dst[r0:r0 + 128, c0:c0 + w_].rearrange("p (a b) -> p a b", b=cw_),
                                            src[r0:r0 + 128, c0:c0 + w_].rearrange("p (a b) -> p a b", b=cw_))])
                    c0 += w_

        def cast_oc(dst, src, KC):
            sv = src.rearrange("(kc p) (oc c) -> oc p kc c", p=128, c=128)
            for oc in range(8):
                S.dma("pool", "cast", [(dst[:, oc, :, :], sv[oc])])

        cast_plain(s_rin, dram["ret_w_in"][0])
        cast_oc(s_rout, dram["ret_w_out"][0], 16)
        for l in range(2):
            cast_plain(s_g[l], dram["ffn_w_gate"][l])
            cast_plain(s_u[l], dram["ffn_w_up"][l])
            cast_oc(s_dw[l], dram["ffn_w_down"][l], NHC)
            if l == 0:
                cast_plain(s_din, dram["dn_w_in"][0])
                cast_oc(s_dout, dram["dn_w_out"][0], 16)

        if stage < 1:
            S.wait_dma_all("sp", "cast")
            return nc
        S.dma("sp", "const", [(c32, d_c32), (c16, d_c16), (pp, d_pp),
                              (rp, d_rp.broadcast_to([128, R_W]))], writes=[c32, c16, pp, rp])
        if stage < 1.1:
            S.wait_dma_all("act", "const")
            return nc
        S.copy("dve", identb, identf)
        S.copy("dve", onesb, onesf)
        if stage < 1.2:
            S._wait(S.E["act"], "dve", 2)
            return nc
        S.wait_dma_all("sp", "cast")
        S.dma("sp", "const2", [(wba, s_din.rearrange("(kc p) n -> p kc n", p=128)[:, :, 6144:6160])], writes=[wba])
        if stage < 1.5:
            S.wait_dma_all("act", "const2")
            return nc
        S.act(nexpa, rp[:, R_ALOG:R_ALOG + 8], AF.Exp)
        S.ts("dve", nexpa, nexpa, -1.0, ALU.mult)
        if stage < 2:
            S.wait_dma_all("act", "const")
            S.wait_dma_all("act", "const2")
            return nc
        Sr = sb.view(o_Sr, F32, [8, 512])
        Srb = sb.view(o_Srb, BF16, [8, 512])
        Sd = sb.view(o_Sd, F32, [8, 256])
        Sdb = sb.view(o_Sdb, BF16, [8, 256])
        for v_ in (Sr, Srb, Sd, Sdb):
            S.memset("pool", v_, 0.0)

        def wlist():
            L = []
            rin = s_rin.rearrange("(kc p) n -> p kc n", p=128)
            for g in range(12):
                L.append(("rin%d" % g, rin[:, :, g * 512:(g + 1) * 512], [8, 512]))
            for g in range(4):
                L.append(("rout%d" % g, s_rout[:, 2 * g:2 * g + 2, :, :], [2, 16, 128]))

            def ffn(l):
                gg = s_g[l].rearrange("(kc p) n -> p kc n", p=128)
                uu = s_u[l].rearrange("(kc p) n -> p kc n", p=128)
                for g in range(6):
                    ncol = min(512, FF - g * 512)
                    L.append(("g%d_%d" % (l, g), gg[:, :, g * 512:g * 512 + ncol], [8, ncol]))
                    L.append(("u%d_%d" % (l, g), uu[:, :, g * 512:g * 512 + ncol], [8, ncol]))
                for oc in range(8):
                    L.append(("dw%d_%d" % (l, oc), s_dw[l][:, oc, :, :], [NHC, 128]))
            ffn(0)
            if n_layers > 1:
                din = s_din.rearrange("(kc p) n -> p kc n", p=128)
                for g in range(12):
                    L.append(("din%d" % g, din[:, :, g * 512:(g + 1) * 512], [8, 512]))
                for g in range(4):
                    L.append(("dout%d" % g, s_dout[:, 2 * g:2 * g + 2, :, :], [2, 16, 128]))
                ffn(1)
            return L

        WL = wlist()
        NW = len(WL)
        wstate = {"issued": 0, "used": 0}
        total_w = NW * NT

        def w_issue():
            i = wstate["issued"]
            name, src, shp = WL[i % NW]
            slot = i % NSLOT
            dst = sb.view(o_ring + slot * SLOT_B, BF16, shp)
            S.dma("sp", "w%d" % slot, [(dst, src)], writes=[dst], max_dma_last_dim=1024)
            wstate["issued"] += 1

        def wnext(name):
            i = wstate["used"]
            nm, src, shp = WL[i % NW]
            assert nm == name, (nm, name)
            while wstate["issued"] < min(total_w, i + NSLOT - 1):
                w_issue()
            wstate["used"] += 1
            return sb.view(o_ring + (i % NSLOT) * SLOT_B, BF16, shp)

        sqb = [sb.view(o_sq + k * T * 2, BF16, [T]) for k in range(2)]
        t0 = sb.view(o_t0, F32, [T])
        t1 = sb.view(o_t1, F32, [T])
        junk = sb.view(o_junk, BF16, [1024])
        small = sb.view(o_small, F32, [64])
        flip = [0]

        def evac(out, in_):
            flip[0] ^= 1
            S.copy("act" if flip[0] else "dve", out, in_)

        def rstd_from(ssq_v, out_v, mean_scale, p0=0, p1=128):
            S.act(out_v, ssq_v, AF.Ln, bias=epsc[p0:p1], scale=mean_scale)
            S.act(out_v, out_v, AF.Exp, scale=-0.5)

        epsc = sb.view(o_small + 256, F32, [1])
        S.memset("pool", epsc, EPS)
        onec = sb.view(o_small + 260, F32, [1])
        S.memset("pool", onec, 1.0)

        def rmsnorm(col0):
            ssq = ps(T * 4, F32, [T])
            for kc in range(8):
                S.act(sqb[kc % 2], hT[:, kc, :], AF.Square)
                S.mm(ssq, onesb, sqb[kc % 2], start=(kc == 0), stop=(kc == 7))
            rstd_from(ssq, t0, 1.0 / D)
            for kc in range(8):
                S.stt(hnT[:, kc, :], hT[:, kc, :], pp[:, P_NCOL + col0 + kc:P_NCOL + col0 + kc + 1], t0, ALU.mult, ALU.mult)

        def proj_fm(wg, ci, out_ps):
            for kc in range(8):
                S.mm(out_ps, wg[:, kc, ci * 128:(ci + 1) * 128], hnT[:, kc, :], start=(kc == 0), stop=(kc == 7))

        def proj_tm(wg, tb, out_ps, ncol=512):
            for kc in range(8):
                S.mm(out_ps, hnT[:, kc, tb * 128:(tb + 1) * 128], wg[:, kc, 0:ncol], start=(kc == 0), stop=(kc == 7))

        sgw = sb.view(o_C, BF16, [NB, 2048])
        ogT = sb.view(o_A, BF16, [16, T])

        def mixer_out(wname):
            for fc in range(16):
                tp = ps(T * 2, BF16, [T])
                for tb in range(NB):
                    S.tr(tp[:, tb * 128:(tb + 1) * 128], sgw[:, tb, fc * 128:(fc + 1) * 128], identb)
                evac(ogT[:, fc, :], tp)
            for g in range(4):
                wg = wnext("%s%d" % (wname, g))
                for o2 in range(2):
                    oc = 2 * g + o2
                    mp = ps(T * 4, F32, [T])
                    for fc in range(16):
                        S.mm(mp, wg[:, o2, fc, :], ogT[:, fc, :], start=(fc == 0), stop=(fc == 15))
                    S.tt("dve", hT[:, oc, :], mp, hT[:, oc, :], ALU.add)

        ffa = sb.view(o_A, BF16, [NHC, T])
        fts = [sb.view(o_ft + k * T * 4, F32, [T]) for k in range(2)]

        def ffn(l):
            rmsnorm(16 + l * 8)
            for g in range(6):
                ncol = min(512, FF - g * 512)
                wgg = wnext("g%d_%d" % (l, g))
                wgu = wnext("u%d_%d" % (l, g))
                for ci in range(ncol // 128):
                    hc = g * 4 + ci
                    gp = ps(T * 4, F32, [T])
                    proj_fm(wgg, ci, gp)
                    up = ps(T * 4, F32, [T])
                    proj_fm(wgu, ci, up)
                    ft = fts[hc % 2]
                    S.act(ft, gp, AF.Silu)
                    S.tt("dve", ffa[:, hc, :], up, ft, ALU.mult)
            for oc in range(8):
                wg = wnext("dw%d_%d" % (l, oc))
                mp = ps(T * 4, F32, [T])
                for hc in range(NHC):
                    S.mm(mp, wg[:, hc, :], ffa[:, hc, :], start=(hc == 0), stop=(hc == NHC - 1))
                S.tt("dve", hT[:, oc, :], mp, hT[:, oc, :], ALU.add)

        gam = _gammas()
        qkT = sb.view(o_A, BF16, [16, T])
        vtok = sb.view(o_B, BF16, [NB, 2048])
        qxiT = sb.view(o_E, BF16, [8, T])
        kz = sb.view(o_F, BF16, [NB, 1024])
        rtmp = [sb.view(o_X + k * T * 4, F32, [T]) for k in range(4)]
        Shb = [sb.view(o_Sh + k * 1024, BF16, [512]) for k in range(2)]
        gnw_row = rp[:, R_GNW:R_GNW + 512]

        def retention():
            rmsnorm(0)
            for g in range(4):
                wg = wnext("rin%d" % g)
                isk = g >= 2
                for hh in range(2):
                    h = (g % 2) * 2 + hh
                    p1 = ps(T * 4, F32, [T])
                    proj_fm(wg, hh * 2, p1)
                    p2 = ps(T * 4, F32, [T])
                    proj_fm(wg, hh * 2 + 1, p2)
                    cs = tab[:, 2 if isk else 0, :]
                    sn = tab[:, 3 if isk else 1, :]
                    dst = qkT[:, (8 if isk else 0) + h * 2, :]
                    dst2 = qkT[:, (8 if isk else 0) + h * 2 + 1, :]
                    S.tt("dve", rtmp[0], p1, cs, ALU.mult)
                    S.tt("dve", rtmp[1], p2, sn, ALU.mult)
                    S.tt("pool", dst, rtmp[0], rtmp[1], ALU.subtract)
                    S.tt("dve", rtmp[2], p1, sn, ALU.mult)
                    S.tt("dve", rtmp[3], p2, cs, ALU.mult)
                    S.tt("pool", dst2, rtmp[2], rtmp[3], ALU.add)
            for g in range(4):
                wg = wnext("rin%d" % (4 + g))
                for tb in range(NB):
                    vp = ps(2048, F32, [512])
                    proj_tm(wg, tb, vp)
                    evac(vtok[:, tb, g * 512:(g + 1) * 512], vp)
            for g in range(4):
                wg = wnext("rin%d" % (8 + g))
                for tb in range(NB):
                    gp = ps(2048, F32, [512])
                    proj_tm(wg, tb, gp)
                    dst = sgw[:, tb, g * 512:(g + 1) * 512]
                    S.act(dst, gp, AF.Silu)
                    S.tt("pool", dst, dst, gnw_row, ALU.mult)
            for c8 in range(8):
                h = c8 // 2
                S.tt("pool", qxiT[:, c8, :], qkT[:, c8, :], c16[:, C_XI + h * T:C_XI + (h + 1) * T], ALU.mult)
            for tb in range(NB):
                for h in range(4):
                    kp = ps(512, BF16, [256])
                    for half in range(2):
                        S.tr(kp[:, half * 128:(half + 1) * 128], qkT[:, 8 + h * 2 + half, tb * 128:(tb + 1) * 128], identb)
                    S.act(kz[:, tb, h * 256:(h + 1) * 256], kp, AF.Copy,
                          scale=c32[:, C_ZETA + tb * 4 + h:C_ZETA + tb * 4 + h + 1])
            for h in range(4):
                Sh = Shb[h % 2]
                offs = []
                o = 0
                for jb in range(NB):
                    N = T - jb * 128
                    sp_ = ps(N * 4, F32, [N])
                    for half in range(2):
                        S.mm(sp_, qkT[:, 8 + h * 2 + half, jb * 128:(jb + 1) * 128], qkT[:, h * 2 + half, jb * 128:T],
                             start=(half == 0), stop=(half == 1))
                    S.tt("dve", Sh[:, o:o + N], sp_, c16[:, C_D + h * NB * 128:C_D + h * NB * 128 + N], ALU.mult)
                    offs.append(o)
                    o += N
                for ib in range(NB):
                    op_ = ps(2048, F32, [512])
                    first = True
                    for jb in range(ib + 1):
                        S.mm(op_, Sh[:, offs[jb] + (ib - jb) * 128: offs[jb] + (ib - jb + 1) * 128],
                             vtok[:, jb, h * 512:(h + 1) * 512], start=first, stop=False)
                        first = False
                    for half in range(2):
                        S.mm(op_, qxiT[:, h * 2 + half, ib * 128:(ib + 1) * 128], Srb[:, h * 2 + half, :],
                             start=False, stop=(half == 1))
                    ssq = small[:, h * 4 + ib:h * 4 + ib + 1]
                    S.act(junk[:, 0:512], op_, AF.Square, accum_out=ssq)
                    rstd_from(ssq, ssq, 1.0 / 512)
                    dst = sgw[:, ib, h * 512:(h + 1) * 512]
                    S.stt(dst, op_, ssq, dst, ALU.mult, ALU.mult)
                for half in range(2):
                    dp = ps(2048, F32, [512])
                    for jb in range(NB):
                        S.mm(dp, kz[:, jb, h * 256 + half * 128:h * 256 + (half + 1) * 128],
                             vtok[:, jb, h * 512:(h + 1) * 512], start=(jb == 0), stop=(jb == NB - 1))
                    sv = Sr[:, h * 2 + half, :]
                    S.stt(sv, sv, float(gam[h] ** T), dp, ALU.mult, ALU.add)
                    S.copy("pool", Srb[:, h * 2 + half, :], sv)
            mixer_out("rout")


        qknT = sb.view(o_A, BF16, [16, T])
        vT = sb.view(o_B, BF16, [16, T])
        cins = [sb.view(o_cin + k * (T + 4) * 4, F32, [T + 4]) for k in range(4)]
        hist = sb.view(o_hist, F32, [32, 4])
        caccs = [sb.view(o_cacc + k * T * 4, F32, [T]) for k in range(4)]
        gts = sb.view(o_gts, F32, [4, NB, 8])
        chv = [sb.view(o_ch + k * 2048, F32, [8, 64]) for k in range(8)]
        NTp = [sb.view(o_p1 + c * 3072, BF16, [8, 64]) for c in range(NCH)]
        qkp = [sb.view(o_p1 + c * 3072 + 1024, BF16, [8, 64]) for c in range(NCH)]
        qgT = [sb.view(o_p1 + c * 3072 + 2048, BF16, [8, 64]) for c in range(NCH)]
        kbg = [sb.view(o_p1 + NCH * 3072 + tb * 4096, BF16, [8, 128]) for tb in range(NB)]
        kdec = [sb.view(o_p1 + NCH * 3072 + tb * 4096 + 2048, BF16, [8, 128]) for tb in range(NB)]
        bcx = [sb.view(o_bcx + k * 1024, BF16, [8, 64]) for k in range(6)]
        ktok = sb.view(o_ktok, BF16, [8, 128])
        vbf = [sb.view(o_vb, BF16, [8, 256]) for k in range(2)]
        vnew = sb.view(o_vnew, BF16, [8, 256])
        nwT = sb.view(o_nwT, BF16, [8, 64])
        egl = sb.view(o_eg, F32, [NCH, 8])
        sm8 = sb.view(o_eg + NCH * 32, F32, [16])
        osq = sb.view(o_osq, BF16, [2048])
        lnqc = sb.view(o_small + 264, F32, [1])
        S.memset("pool", lnqc, float(np.log(128.0 ** -0.5)))
        S.memset("pool", hist, 0.0)
        S.memset("pool", vnew, 0.0)
        for c in range(NCH):
            S.memset("pool", NTp[c], 0.0)
            S.memset("pool", qkp[c], 0.0)
        dnw_row = rp[:, R_DNW:R_DNW + 256]
        TLE = c32[:, C_TLE:C_TLE + 64]
        TGT = c32[:, C_TGT:C_TGT + 64]

        def deltanet(n):
            rmsnorm(8)
            for g in range(8):
                wg = wnext("din%d" % g)
                for ci in range(4):
                    cc = g * 4 + ci
                    pp_ = ps(T * 4, F32, [T])
                    proj_fm(wg, ci, pp_)
                    cin = cins[cc % 4]
                    acc = caccs[cc % 4]
                    S.copy("pool", cin[:, 0:3], hist[:, cc, 0:3])
                    S.act(cin[:, 3:3 + T], pp_, AF.Copy)
                    if n == 0:
                        S.memset("pool", cin[:, 3:3 + LPAD], 0.0)
                    S.copy("pool", hist[:, cc, 0:3], cin[:, T:T + 3])
                    cw = lambda t: pp[:, P_CONV + cc * 4 + t:P_CONV + cc * 4 + t + 1]
                    S.ts("dve", acc, cin[:, 3:3 + T], cw(3), ALU.mult)
                    for t in (2, 1, 0):
                        S.stt(acc, cin[:, t:t + T], cw(t), acc, ALU.mult, ALU.add)
                    if cc < 16:
                        S.act(qknT[:, cc, :], acc, AF.Silu)
                    else:
                        S.act(vT[:, cc - 16, :], acc, AF.Silu)
                if g == 1 or g == 3:
                    base = 0 if g == 1 else 8
                    blk = qknT[:, base:base + 8, :]
                    sqbig = sb.view(o_osq, BF16, [8, T])
                    S.act(sqbig, blk, AF.Square)
                    ssq8 = ps(8 * T * 4, F32, [8, T])
                    for h in range(8):
                        S.mm(ssq8[:, h, :], onesb, sqbig[:, h, :])
                    rst = sb.view(o_ch, F32, [8, T])
                    S.act(rst, ssq8, AF.Ln, bias=epsc, scale=1.0)
                    if base == 0:
                        S.act(rst, rst, AF.Exp, bias=lnqc, scale=-0.5)
                    else:
                        S.act(rst, rst, AF.Exp, scale=-0.5)
                    S.tt("dve", blk, blk, rst, ALU.mult)
            for g in range(4):
                wg = wnext("din%d" % (8 + g))
                for tb in range(NB):
                    gp = ps(2048, F32, [512])
                    proj_tm(wg, tb, gp)
                    dst = sgw[:, tb, g * 512:(g + 1) * 512]
                    S.act(dst, gp, AF.Silu)
                    d3 = dst.rr("p (h v) -> p h v", h=2)
                    S.tt("pool", d3, d3, dnw_row.us(1).bc([128, 2, 256]), ALU.mult)
            for tb in range(NB):
                bp = ps(64, F32, [16])
                for kc in range(8):
                    S.mm(bp, hnT[:, kc, tb * 128:(tb + 1) * 128], wba[:, kc, :], start=(kc == 0), stop=(kc == 7))
                S.act(gts[:, 1, tb, :], bp[:, 0:8], AF.Sigmoid)
                S.tt("dve", gts[:, 3, tb, :], bp[:, 8:16], rp[:, R_DTB:R_DTB + 8], ALU.add)
                S.act(gts[:, 3, tb, :], gts[:, 3, tb, :], AF.Exp)
                S.act(gts[:, 3, tb, :], gts[:, 3, tb, :], AF.Ln, bias=onec, scale=1.0)
                S.tt("dve", gts[:, 0, tb, :], gts[:, 3, tb, :], nexpa, ALU.mult)
                if n == 0:
                    vcol = c32[:, C_VAL + tb:C_VAL + tb + 1]
                    S.ts("dve", gts[:, 0, tb, :], gts[:, 0, tb, :], vcol, ALU.mult)
                    S.ts("dve", gts[:, 1, tb, :], gts[:, 1, tb, :], vcol, ALU.mult)
                S.ts("dve", gts[:, 2, tb, :], gts[:, 1, tb, :], -1.0, ALU.mult)
            for tb in range(NB):
                ktp = ps(2048, BF16, [8, 128])
                for h in range(8):
                    S.tr(ktp[:, h, :], qknT[:, 8 + h, tb * 128:(tb + 1) * 128], identb)
                S.copy("act", ktok, ktp)
                for half in range(2):
                    c = tb * 2 + half
                    P0, P1 = half * 64, half * 64 + 64
                    tok = slice(c * 64, (c + 1) * 64)
                    g_c = gts[P0:P1, 0, tb, :]
                    beta_c = gts[P0:P1, 1, tb, :]
                    nbeta_c = gts[P0:P1, 2, tb, :]
                    gtri1, decT, decS, tmul, egrow, gtri2 = chv[0][P0:P1], chv[1][P0:P1], chv[2][P0:P1], chv[3][P0:P1], chv[4], chv[5][P0:P1]
                    S.tt("pool", gtri1, g_c.us(2).bc([64, 8, 64]), TLE[P0:P1].us(1).bc([64, 8, 64]), ALU.mult)
                    S.tt("pool", gtri2, g_c.us(2).bc([64, 8, 64]), TGT[P0:P1].us(1).bc([64, 8, 64]), ALU.mult)
                    g1f = gtri1.rr("p h i -> p (h i)")
                    g2f = gtri2.rr("p h i -> p (h i)")
                    id64 = identf[P0:P1, P0:P1]
                    gr_ps = ps(2048, F32, [512])
                    S.mm(gr_ps, onesf[P0:P1, :], g1f)
                    dT_ps = ps(2048, F32, [512], P0, P1)
                    S.mm(dT_ps, TGT[P0:P1], g1f, start=True, stop=False)
                    S.mm(dT_ps, id64, c32[P0:P1, C_MT:C_MT + 512], start=False, stop=True)
                    dS_ps = ps(2048, F32, [512], P0, P1)
                    S.mm(dS_ps, TLE[P0:P1], g2f, start=True, stop=False)
                    S.mm(dS_ps, id64, c32[P0:P1, C_MS:C_MS + 512], start=False, stop=True)
                    gc_ps = ps(32, F32, [8], P0, P1)
                    S.mm(gc_ps, TLE[P0:P1], g_c)
                    S.act(decT.rr("p h i -> p (h i)"), dT_ps, AF.Exp)
                    S.act(decS.rr("p h i -> p (h i)"), dS_ps, AF.Exp)
                    S.act(egrow.rr("p h i -> p (h i)"), gr_ps, AF.Exp)
                    gcol = sm8[P0:P1, 0:8]
                    dcol = sm8[P0:P1, 8:16]
                    S.copy("act", gcol, gc_ps)
                    S.tt("dve", dcol, gr_ps[P0:P1].rr("p (h i) -> p h i", h=8)[:, :, 63], gcol, ALU.subtract)
                    S.act(dcol, dcol, AF.Exp)
                    S.act(gcol, gcol, AF.Exp)
                    S.tt("dve", gcol, gcol, beta_c, ALU.mult)
                    S.copy("pool", egl[:, c, :], egrow[:, :, 63])
                    kk_ps = ps(2048, F32, [8, 64], P0, P1)
                    qk_ps = ps(2048, F32, [8, 64], P0, P1)
                    for h in range(8):
                        S.mm(kk_ps[:, h, :], qknT[:, 8 + h, tok], qknT[:, 8 + h, tok])
                    for h in range(8):
                        S.mm(qk_ps[:, h, :], qknT[:, 8 + h, tok], qknT[:, h, tok])
                    S.tt("pool", tmul, decS, nbeta_c.us(2).bc([64, 8, 64]), ALU.mult)
                    B, C, X = bcx[0][P0:P1], bcx[1][P0:P1], bcx[2][P0:P1]
                    S.tt("dve", B, kk_ps, tmul, ALU.mult)
                    S.tt("dve", qkp[c][P0:P1], qk_ps, decT, ALU.mult)
                    ct_ps = ps(1024, BF16, [8, 64], P0, P1)
                    idb64 = identb[P0:P1, P0:P1]
                    for h in range(8):
                        S.tr(ct_ps[:, h, :], B[:, h, :], idb64)
                    S.copy("act", C, ct_ps)
                    S.tt("dve", X, ct_ps, id64.us(1).bc([64, 8, 64]), ALU.add)
                    for k in range(1, 6):
                        Bn, Cn, Xn = bcx[3 * (k % 2)][P0:P1], bcx[3 * (k % 2) + 1][P0:P1], bcx[3 * (k % 2) + 2][P0:P1]
                        if k == 5:
                            Xn = NTp[c][P0:P1]
                        b_ps = ps(2048, F32, [8, 64], P0, P1)
                        for h in range(8):
                            S.mm(b_ps[:, h, :], C[:, h, :], B[:, h, :])
                        if k < 5:
                            c_ps = ps(2048, F32, [8, 64], P0, P1)
                            for h in range(8):
                                S.mm(c_ps[:, h, :], B[:, h, :], C[:, h, :])
                        S.copy("act", Bn, b_ps)
                        if k < 5:
                            S.copy("dve", Cn, c_ps)
                        x_ps = ps(2048, F32, [8, 64], P0, P1)
                        for h in range(8):
                            S.mm(x_ps[:, h, :], Bn[:, h, :], X[:, h, :])
                        S.tt("dve", Xn, x_ps, X, ALU.add)
                        B, C, X = Bn, Cn, Xn
                    S.tt("pool", kbg[tb][P0:P1], ktok[P0:P1], gcol.us(2).bc([64, 8, 128]), ALU.mult)
                    S.tt("pool", kdec[tb][P0:P1], ktok[P0:P1], dcol.us(2).bc([64, 8, 128]), ALU.mult)
                    S.tt("pool", qgT[c], qknT[:, 0:8, tok], egrow, ALU.mult)
            for tb in range(NB):
                vtp = ps(4096, BF16, [16, 128])
                for vc in range(16):
                    S.tr(vtp[:, vc, :], vT[:, vc, tb * 128:(tb + 1) * 128], identb)
                vb = vbf[tb % 2]
                S.tt("dve", vb, vtp.rr("p (h a) c -> p h (a c)", h=8), gts[:, 1, tb, :].us(2).bc([128, 8, 256]), ALU.mult)
                for half in range(2):
                    c = tb * 2 + half
                    P0, P1 = half * 64, half * 64 + 64
                    w_ps = ps(2048, F32, [8, 64])
                    for h in range(8):
                        S.mm(w_ps[:, h, :], kbg[tb][P0:P1, h, :], NTp[c][P0:P1, h, :])
                    S.act(nwT, w_ps, AF.Copy, scale=-1.0)
                    vn_ps = ps(8192, F32, [8, 256], P0, P1)
                    for h in range(8):
                        S.mm(vn_ps[:, h, :], NTp[c][:, h, :], vb[:, h, :], start=True, stop=False)
                        S.mm(vn_ps[:, h, :], nwT[:, h, :], Sdb[:, h, :], start=False, stop=True)
                    S.copy("act", vnew[P0:P1, 0:4, :], vn_ps[:, 0:4, :])
                    S.copy("dve", vnew[P0:P1, 4:8, :], vn_ps[:, 4:8, :])
                    d_ps = ps(8192, F32, [8, 256])
                    for h in range(8):
                        S.mm(d_ps[:, h, :], kdec[tb][P0:P1, h, :], vnew[P0:P1, h, :])
                    o_ps = ps(8192, F32, [8, 256], P0, P1)
                    for h in range(8):
                        S.mm(o_ps[:, h, :], qkp[c][:, h, :], vnew[:, h, :], start=True, stop=False)
                        S.mm(o_ps[:, h, :], qgT[c][:, h, :], Sdb[:, h, :], start=False, stop=True)
                    for h in range(8):
                        S.stt(Sd[:, h, :], Sd[:, h, :], egl[:, c, h:h + 1], d_ps[:, h, :], ALU.mult, ALU.add)
                        S.copy("pool" if h % 2 else "act", Sdb[:, h, :], Sd[:, h, :])
                    S.act(osq[P0:P1], o_ps.rr("p h v -> p (h v)"), AF.Square)
                    ss8 = sm8[P0:P1, 0:8]
                    S.reduce_add(ss8, osq[P0:P1].rr("p (h v) -> p h v", h=8))
                    rstd_from(ss8, ss8, 1.0 / 256, P0, P1)
                    for h in range(8):
                        dst = sgw[P0:P1, tb, h * 256:(h + 1) * 256]
                        S.stt(dst, o_ps[:, h, :], ss8[:, h:h + 1], dst, ALU.mult, ALU.mult)
            mixer_out("dout")

        finw_row = rp[:, R_FIN:R_FIN + 1024]
        yst = sb.view(o_A, F32, [NB, 1024])

        def load_x(n):
            if n == 0:
                S.memset("pool", xin, 0.0)
                S.dma("sp", "xin", [(xin[112:128, NB - 1, :], dram["meta_tokens"])], writes=[xin])
            else:
                S.dma("sp", "xin", [(xin, x[(n - 1) * T:n * T, :].rearrange("(tb p) d -> p tb d", p=128))], writes=[xin])

        def load_tab(n):
            S.dma("sp", "tab", [(tab, d_tab[n])], writes=[tab])

        for n in range(NT):
            if n == 0:
                load_x(0)
                load_tab(0)
            for kc in range(8):
                tp = ps(T * 4, F32, [T])
                for tb in range(NB):
                    S.tr(tp[:, tb * 128:(tb + 1) * 128], xin[:, tb, kc * 128:(kc + 1) * 128], identf)
                evac(hT[:, kc, :], tp)
            if stage < 3:
                S.dma("sp", "yout", [(y[0:128, :], hT.rr("p a b -> p (a b)")[:, 0:1024])], reads=[hT])
                break
            retention()
            if n + 1 < NT:
                load_tab(n + 1)
            if stage < 4:
                S.dma("sp", "yout", [(y[0:128, :], hT.rr("p a b -> p (a b)")[:, 0:1024])], reads=[hT])
                break
            ffn(0)
            if n_layers > 1:
                deltanet(n)
                if n + 1 < NT:
                    load_x(n + 1)
                ffn(1)
            if n >= 1:
                for tb in range(NB):
                    fp = ps(4096, F32, [1024])
                    for kc in range(8):
                        S.tr(fp[:, kc * 128:(kc + 1) * 128], hT[:, kc, tb * 128:(tb + 1) * 128], identf)
                    ssq = small[:, 32 + tb:33 + tb]
                    S.act(junk, fp, AF.Square, accum_out=ssq)
                    rstd_from(ssq, ssq, 1.0 / D)
                    S.stt(yst[:, tb, :], fp, ssq, finw_row, ALU.mult, ALU.mult)
                S.dma("sp", "yout", [(y[(n - 1) * T:n * T, :].rearrange("(tb p) d -> p tb d", p=128), yst)], reads=[yst])
        assert stage < 99 or wstate["used"] == total_w
        if "yout" in S.dsem:
            S.wait_dma_all("sp", "yout")
        for nm, ap_ in dbg_out.items():
            pass
        print("instructions:", S.n_ins, "arena bytes:", NBYTES)
    return nc


def make_in_map(inp, x_core, NT):
    c32, c16, tab = host_consts(NT)
    f = lambda a: np.ascontiguousarray(np.asarray(a, dtype=np.float32))
    m = {"x": f(x_core)}
    for name, shp in IN_SPECS:
        m[name] = f(inp[name]).reshape(shp)
    m["c32"] = c32
    m["c16"] = c16
    m["tab"] = tab
    rp = np.zeros((1, R_W), np.float32)
    rp[0, R_FIN:R_FIN + 1024] = f(inp["final_norm_w"])
    rp[0, R_GNW:R_GNW + 512] = f(inp["ret_gn_w"]).reshape(-1)
    rp[0, R_DNW:R_DNW + 256] = f(inp["dn_norm_w"]).reshape(-1)
    rp[0, R_ALOG:R_ALOG + 8] = f(inp["dn_a_log"]).reshape(-1)
    rp[0, R_DTB:R_DTB + 8] = f(inp["dn_dt_bias"]).reshape(-1)
    m["rpack"] = rp
    pp = np.zeros((128, P_W), np.float32)
    nv = [f(inp["mix_norm_w"])[0], f(inp["mix_norm_w"])[1], f(inp["ffn_norm_w"])[0], f(inp["ffn_norm_w"])[1]]
    for vi, v in enumerate(nv):
        pp[:, P_NCOL + vi * 8:P_NCOL + vi * 8 + 8] = v.reshape(8, 128).T
    cw = f(inp["dn_conv_w"])[0]
    pp[:, P_CONV:P_CONV + 128] = cw.reshape(4, 32, 128).transpose(2, 1, 0).reshape(128, 128)
    m["ppack"] = pp
    return m


_NC_CACHE = {}


def kernel(x, meta_tokens, mix_norm_w, ffn_norm_w, ret_w_in, ret_gn_w, ret_w_out,
           dn_w_in, dn_conv_w, dn_a_log, dn_dt_bias, dn_norm_w, dn_w_out,
           ffn_w_gate, ffn_w_up, ffn_w_down, final_norm_w):
    x = np.asarray(x)
    B, SEQ, _ = x.shape
    assert SEQ % T == 0
    NT = SEQ // T + 1
    inp = dict(meta_tokens=meta_tokens, mix_norm_w=mix_norm_w, ffn_norm_w=ffn_norm_w, ret_w_in=ret_w_in,
               ret_gn_w=ret_gn_w, ret_w_out=ret_w_out, dn_w_in=dn_w_in, dn_conv_w=dn_conv_w,
               dn_a_log=dn_a_log, dn_dt_bias=dn_dt_bias, dn_norm_w=dn_norm_w, dn_w_out=dn_w_out,
               ffn_w_gate=ffn_w_gate, ffn_w_up=ffn_w_up, ffn_w_down=ffn_w_down, final_norm_w=final_norm_w)
    inp = {k: np.asarray(v) for k, v in inp.items()}
    if NT not in _NC_CACHE:
        _NC_CACHE[NT] = build_program(NT, n_layers=2)
    nc = _NC_CACHE[NT]
    base = make_in_map(inp, x[0], NT)
    in_maps = []
    for b in range(B):
        m = dict(base)
        m["x"] = np.ascontiguousarray(x[b], dtype=np.float32)
        in_maps.append(m)
    res = run_bass_kernel_spmd(nc, in_maps, core_ids=list(range(B)))
    return np.stack([np.asarray(r["y"]) for r in res.results], axis=0).astype(np.float32)
```
